# Optimizing a Trainium2 kernel written in Bass

```python
import jax, jax.numpy as jnp
from jax import lax
import numpy as np

D_MODEL = 2048
BATCH = 8
SEQ = 4096
DEPTH = 1

MEM_TOKENS = 256

ML_HEADS = 6
ML_DQK = 128
ML_DV = 256
ML_CONV = 4
ML_CHUNK = 64
RET_HEADS = 6
RET_DQK = 128
RET_DV = 256
RET_CHUNK = 64
XA_HEADS = 4
XA_DH = 256

ROPE_BASE = 10000.0
EPS = 1e-6
N_BRANCH = 3

ML_QK = ML_HEADS * ML_DQK
ML_V = ML_HEADS * ML_DV
RET_QK = RET_HEADS * RET_DQK
RET_V = RET_HEADS * RET_DV
XA_W = XA_HEADS * XA_DH

IN_SIZES = (ML_QK, ML_QK, ML_V, ML_V, ML_V, ML_HEADS, ML_HEADS,
            RET_QK, RET_QK, RET_V, RET_V, XA_W, XA_W, N_BRANCH * D_MODEL)
N_IN = sum(IN_SIZES)

kernel_name = "hybrid_mlstm_retention_memxattn_gated"


def rmsnorm(x, g):
    xf = x.astype(jnp.float32)
    y = xf * lax.rsqrt(jnp.mean(xf * xf, axis=-1, keepdims=True) + EPS)
    return (y * g.astype(jnp.float32)).astype(x.dtype)


def head_layernorm(t, g):
    B, S, H, D = t.shape
    tf = t.astype(jnp.float32)
    mu = jnp.mean(tf, axis=-1, keepdims=True)
    var = jnp.mean(jnp.square(tf - mu), axis=-1, keepdims=True)
    y = ((tf - mu) * lax.rsqrt(var + EPS)).reshape(B, S, H * D)
    return y * g.astype(jnp.float32)


def split_heads(t, n_heads):
    B, S, W = t.shape
    return t.reshape(B, S, n_heads, W // n_heads)


def rope(t, positions):
    half = t.shape[-1] // 2
    freqs = ROPE_BASE ** (-jnp.arange(half, dtype=jnp.float32) / half)
    ang = positions.astype(jnp.float32)[..., None] * freqs
    cos = jnp.cos(ang)[:, :, None, :]
    sin = jnp.sin(ang)[:, :, None, :]
    tf = t.astype(jnp.float32)
    t1, t2 = tf[..., :half], tf[..., half:]
    return jnp.concatenate([t1 * cos - t2 * sin, t1 * sin + t2 * cos], axis=-1)


def causal_dwconv(u, w, b):
    K = w.shape[0]
    S = u.shape[1]
    up = jnp.pad(u, ((0, 0), (K - 1, 0), (0, 0)))
    y = b
    for k in range(K):
        y = y + up[:, k:k + S] * w[k]
    return y


def to_chunks(t, L):
    B, S, H, D = t.shape
    return t.reshape(B, S // L, L, H, D).transpose(1, 0, 3, 2, 4)


def from_chunks(t):
    NC, B, H, L, D = t.shape
    return t.transpose(1, 0, 3, 2, 4).reshape(B, NC * L, H, D)


def mlstm_chunkwise(q, k, v, i_pre, f_pre):
    B, S, H, Dk = q.shape
    Dv = v.shape[-1]
    L = ML_CHUNK
    NC = S // L
    qc = to_chunks(q.astype(jnp.float32) * (Dk ** -0.5), L)
    kc = to_chunks(k.astype(jnp.float32), L)
    vc = to_chunks(v.astype(jnp.float32), L)
    ic = i_pre.astype(jnp.float32).reshape(B, NC, L, H).transpose(1, 0, 3, 2)
    lfc = jax.nn.log_sigmoid(f_pre.astype(jnp.float32)).reshape(B, NC, L, H).transpose(1, 0, 3, 2)
    causal = jnp.tril(jnp.ones((L, L), dtype=bool))

    def step(carry, xs):
        C, n, m = carry
        qb, kb, vb, ib, lfb = xs
        b = jnp.cumsum(lfb, axis=-1)
        g = b[..., -1]
        log_d = jnp.where(causal, b[..., :, None] - b[..., None, :] + ib[..., None, :], -jnp.inf)
        log_inter = b + m[..., None]
        m_row = jnp.maximum(log_inter, jnp.max(log_d, axis=-1))
        d = jnp.exp(log_d - m_row[..., None])
        s = jnp.einsum('bhld,bhsd->bhls', qb, kb) * d
        inter = jnp.exp(log_inter - m_row)
        num = jnp.einsum('bhls,bhsv->bhlv', s, vb) + inter[..., None] * jnp.einsum('bhld,bhdv->bhlv', qb, C)
        den = jnp.sum(s, axis=-1) + inter * jnp.einsum('bhld,bhd->bhl', qb, n)
        den = jnp.maximum(jnp.abs(den), jnp.exp(-m_row))
        h = num / den[..., None]
        log_w = g[..., None] - b + ib
        m_new = jnp.maximum(g + m, jnp.max(log_w, axis=-1))
        w = jnp.exp(log_w - m_new[..., None])
        decay = jnp.exp(g + m - m_new)
        kw = kb * w[..., None]
        C_new = decay[..., None, None] * C + jnp.einsum('bhsd,bhsv->bhdv', kw, vb)
        n_new = decay[..., None] * n + jnp.sum(kw, axis=2)
        return (C_new, n_new, m_new), h

    init = (jnp.zeros((B, H, Dk, Dv), jnp.float32),
            jnp.zeros((B, H, Dk), jnp.float32),
            jnp.zeros((B, H), jnp.float32))
    _, hc = lax.scan(step, init, (qc, kc, vc, ic, lfc))
    return from_chunks(hc)


def retention_chunkwise(q, k, v):
    B, S, H, Dk = q.shape
    Dv = v.shape[-1]
    L = RET_CHUNK
    log_gamma = jnp.asarray(np.log(1.0 - 2.0 ** (-5.0 - np.arange(H))), dtype=jnp.float32)
    pos = jnp.arange(L, dtype=jnp.float32)
    causal = jnp.tril(jnp.ones((L, L), dtype=bool))
    intra = jnp.where(causal, jnp.exp((pos[:, None] - pos[None, :]) * log_gamma[:, None, None]), 0.0)
    q_decay = jnp.exp((pos + 1.0) * log_gamma[:, None])
    k_decay = jnp.exp((L - 1.0 - pos) * log_gamma[:, None])
    chunk_decay = jnp.exp(L * log_gamma)
    qc = to_chunks(q.astype(jnp.float32), L)
    kc = to_chunks(k.astype(jnp.float32) * (Dk ** -0.5), L)
    vc = to_chunks(v.astype(jnp.float32), L)

    def step(R, xs):
        qb, kb, vb = xs
        s = jnp.einsum('bhld,bhsd->bhls', qb, kb) * intra
        o = jnp.einsum('bhls,bhsv->bhlv', s, vb) + q_decay[..., None] * jnp.einsum('bhld,bhdv->bhlv', qb, R)
        R_new = chunk_decay[:, None, None] * R + jnp.einsum('bhsd,bhsv->bhdv', kb * k_decay[..., None], vb)
        return R_new, o

    _, oc = lax.scan(step, jnp.zeros((B, H, Dk, Dv), jnp.float32), (qc, kc, vc))
    return from_chunks(oc)


def memory_cross_attention(q, mk, mv):
    B, S, H, D = q.shape
    scores = jnp.einsum('bshd,bmhd->bhsm', q.astype(jnp.float32), mk.astype(jnp.float32)) * (D ** -0.5)
    p = jax.nn.softmax(scores, axis=-1)
    o = jnp.einsum('bhsm,bmhd->bshd', p, mv.astype(jnp.float32))
    return o.reshape(B, S, H * D)


def setup_inputs(seed: int = 0) -> dict:
    key = jax.random.key(seed)
    ks = jax.random.split(key, 17)
    f32 = jnp.float32

    def nrm(k, shape, scale):
        return jax.random.normal(k, shape, f32) * scale

    x = nrm(ks[0], (BATCH, SEQ, D_MODEL), 1.0)
    mem = nrm(ks[1], (BATCH, MEM_TOKENS, D_MODEL), 1.0)
    positions = jnp.broadcast_to(jnp.arange(SEQ, dtype=jnp.int32), (BATCH, SEQ))
    ln_g = 1.0 + nrm(ks[2], (DEPTH, D_MODEL), 0.02)
    mem_ln_g = 1.0 + nrm(ks[3], (DEPTH, D_MODEL), 0.02)
    w_in = nrm(ks[4], (DEPTH, D_MODEL, N_IN), D_MODEL ** -0.5)
    f_off = sum(IN_SIZES[:6])
    b_in = nrm(ks[5], (DEPTH, N_IN), 0.01)
    b_in = b_in.at[:, f_off:f_off + ML_HEADS].add(jnp.linspace(3.0, 6.0, ML_HEADS, dtype=f32))
    conv_w = nrm(ks[6], (DEPTH, ML_CONV, 2 * ML_QK), ML_CONV ** -0.5)
    conv_b = nrm(ks[7], (DEPTH, 2 * ML_QK), 0.01)
    ml_hnorm_g = 1.0 + nrm(ks[8], (DEPTH, ML_V), 0.02)
    ret_hnorm_g = 1.0 + nrm(ks[9], (DEPTH, RET_V), 0.02)
    w_mem_kv = nrm(ks[10], (DEPTH, D_MODEL, 2 * XA_W), D_MODEL ** -0.5)
    w_br_ml = nrm(ks[11], (DEPTH, ML_V, D_MODEL), ML_V ** -0.5)
    w_br_ret = nrm(ks[12], (DEPTH, RET_V, D_MODEL), RET_V ** -0.5)
    w_br_xa = nrm(ks[13], (DEPTH, XA_W, D_MODEL), XA_W ** -0.5)
    w_out = nrm(ks[14], (DEPTH, D_MODEL, D_MODEL), D_MODEL ** -0.5)
    final_g = 1.0 + nrm(ks[15], (D_MODEL,), 0.02)
    return {"x": x, "mem": mem, "positions": positions, "ln_g": ln_g, "mem_ln_g": mem_ln_g,
            "w_in": w_in, "b_in": b_in, "conv_w": conv_w, "conv_b": conv_b,
            "ml_hnorm_g": ml_hnorm_g, "ret_hnorm_g": ret_hnorm_g, "w_mem_kv": w_mem_kv,
            "w_br_ml": w_br_ml, "w_br_ret": w_br_ret, "w_br_xa": w_br_xa, "w_out": w_out,
            "final_g": final_g}


def reference(x, mem, positions, ln_g, mem_ln_g, w_in, b_in, conv_w, conv_b, ml_hnorm_g,
              ret_hnorm_g, w_mem_kv, w_br_ml, w_br_ret, w_br_xa, w_out, final_g):
    B, S, _ = x.shape
    split_at = np.cumsum(IN_SIZES)[:-1].tolist()
    for layer in range(DEPTH):
        h = rmsnorm(x, ln_g[layer])
        proj = h @ w_in[layer] + b_in[layer]
        (ml_q, ml_k, ml_v, ml_o, ml_z, ml_i, ml_f,
         rt_q, rt_k, rt_v, rt_z, xa_q, xa_z, gate_pre) = jnp.split(proj, split_at, axis=-1)

        qk = jax.nn.silu(causal_dwconv(jnp.concatenate([ml_q, ml_k], axis=-1), conv_w[layer], conv_b[layer]))
        ml_qc, ml_kc = jnp.split(qk, 2, axis=-1)
        ml_h = mlstm_chunkwise(split_heads(ml_qc, ML_HEADS), split_heads(ml_kc, ML_HEADS),
                               split_heads(ml_v, ML_HEADS), ml_i, ml_f)
        ml_out = (head_layernorm(ml_h, ml_hnorm_g[layer]) * jax.nn.sigmoid(ml_o.astype(jnp.float32))
                  * jax.nn.silu(ml_z.astype(jnp.float32))).astype(x.dtype)

        rq = rope(split_heads(rt_q, RET_HEADS), positions)
        rk = rope(split_heads(rt_k, RET_HEADS), positions)
        rt_h = retention_chunkwise(rq, rk, split_heads(rt_v, RET_HEADS))
        rt_out = (head_layernorm(rt_h, ret_hnorm_g[layer])
                  * jax.nn.silu(rt_z.astype(jnp.float32))).astype(x.dtype)

        mn = rmsnorm(mem, mem_ln_g[layer])
        mk, mv = jnp.split(mn @ w_mem_kv[layer], 2, axis=-1)
        xa_h = memory_cross_attention(split_heads(xa_q, XA_HEADS), split_heads(mk, XA_HEADS),
                                      split_heads(mv, XA_HEADS))
        xa_out = (xa_h * jax.nn.silu(xa_z.astype(jnp.float32))).astype(x.dtype)

        gates = jax.nn.sigmoid(gate_pre).reshape(B, S, N_BRANCH, D_MODEL)
        merged = (gates[:, :, 0] * (ml_out @ w_br_ml[layer])
                  + gates[:, :, 1] * (rt_out @ w_br_ret[layer])
                  + gates[:, :, 2] * (xa_out @ w_br_xa[layer]))
        x = x + merged @ w_out[layer]
    return rmsnorm(x, final_g)
```

```python
import contextlib
import math
import numpy as np
import ml_dtypes
import concourse.bass as bass
import concourse.mybir as mybir
from concourse.bass_utils import run_bass_kernel_spmd

F32 = mybir.dt.float32
BF16 = mybir.dt.bfloat16
I32 = mybir.dt.int32
AF = mybir.ActivationFunctionType
ALU = mybir.AluOpType
AX = mybir.AxisListType

ENGS = ['pe', 'act', 'dve', 'pool', 'sp']
EPOCH = 30000
BIG = 10 ** 9

D = 2048
NIN = 18956
EPS = 1e-6
T = 512
NB = 4
KC = 16
O_MLQ, O_MLK, O_MLV, O_MLO, O_MLZ, O_MLI, O_MLF = 0, 768, 1536, 3072, 4608, 6144, 6150
O_RTQ, O_RTK, O_RTV, O_RTZ, O_XAQ, O_XAZ, O_G = 6156, 6924, 7692, 9228, 10764, 11788, 12812
FB_MLQ, FB_MLK, FB_MLO, FB_MLZ, FB_RTZ, FB_XAQ, FB_XAZ, FB_G = 0, 6, 12, 24, 36, 48, 56, 64
NFB = 112
FM_OFFS = ([O_MLQ + i * 128 for i in range(6)] + [O_MLK + i * 128 for i in range(6)]
           + [O_MLO + i * 128 for i in range(12)] + [O_MLZ + i * 128 for i in range(12)]
           + [O_RTZ + i * 128 for i in range(12)] + [O_XAQ + i * 128 for i in range(8)]
           + [O_XAZ + i * 128 for i in range(8)] + [O_G + i * 128 for i in range(48)])
TM_SEGS = ([(0, O_MLV, 1536)]
           + [(1536 + h * 256, O_RTQ + h * 128, 128) for h in range(6)]
           + [(1536 + h * 256 + 128, O_RTK + h * 128, 128) for h in range(6)]
           + [(3072, O_RTV, 1536)])
NTM = 4608
C_BFM = 0
C_HBFM = 112
C_CONVW = 224
C_CONVB = 272
C_MLG = 284
C_RTG = 296
C_GCOL = 308
C_MGCOL = 324
C_BI = 340
C_BF = 341
C_KDEC = 342
C_CD = 348
NCOLS = 354

PI = math.pi
TWO_PI_HI = 6.28125
TWO_PI_LO = 2.0 * math.pi - 6.28125
PI_CLAMP = 3.1415925
ROPE_ENG = 'dve'
OFF_PT = 13312
OFF_NT = OFF_PT + 12800
OFF_ET = OFF_NT + 3584
OFF_L0 = OFF_ET + 3072
LIVE_SZ = 8192


def CALL(name, *args, **kwargs):
    def f(e):
        return getattr(e, name)(*args, **kwargs)
    return f


class Res:
    __slots__ = ('name', 'w', 'r')

    def __init__(self, name=''):
        self.name = name
        self.w = None
        self.r = {}


class Prog:
    def __init__(self, nc):
        self.nc = nc
        self.ops = {e: [] for e in ENGS}
        self.seen = {e: {} for e in ENGS}
        self.dma_cnt = []
        self.dma_total_only = []

    def new_dma_sem(self, total_only=False):
        self.dma_cnt.append(0)
        self.dma_total_only.append(total_only)
        return len(self.dma_cnt) - 1

    def _dep(self, e, ev, waits):
        key, val = ev
        if key == e and e == 'pe':
            return
        if self.seen[e].get(key, 0) >= val:
            return
        if waits.get(key, 0) < val:
            waits[key] = val

    def _commit(self, e, waits):
        for k, v in waits.items():
            self.seen[e][k] = v
            if not isinstance(k, tuple):
                self.ops[k][v - 1]['inc'] = True

    def op(self, e, fn, reads=(), writes=(), dma_sem=None):
        waits = {}
        for r in reads:
            if r.w is not None:
                self._dep(e, r.w, waits)
        for w in writes:
            if w.w is not None:
                self._dep(e, w.w, waits)
            for k, v in w.r.items():
                self._dep(e, (k, v), waits)
        self._commit(e, waits)
        idx = len(self.ops[e])
        if dma_sem is None:
            ev = (e, idx + 1)
        else:
            self.dma_cnt[dma_sem] += 1
            ev = (('d', dma_sem), BIG if self.dma_total_only[dma_sem] else 16 * self.dma_cnt[dma_sem])
        self.ops[e].append(dict(waits=waits, fn=fn, dma=dma_sem, inc=False))
        for r in reads:
            if r.r.get(ev[0], 0) < ev[1]:
                r.r[ev[0]] = ev[1]
        for w in writes:
            w.w = ev
            w.r = {}
        return ev

    def wait_events(self, e, events):
        waits = {}
        for ev in events:
            self._dep(e, ev, waits)
        self._commit(e, waits)
        self.ops[e].append(dict(waits=waits, fn=None, dma=None, inc=False))

    def barrier(self, engines=('pe', 'act', 'dve', 'pool')):
        evs = []
        for f in engines:
            n = len(self.ops[f])
            while n > 0 and (self.ops[f][n - 1]['fn'] is None or self.ops[f][n - 1]['dma'] is not None):
                n -= 1
            if n > 0:
                evs.append((f, n))
        for e in engines:
            self.wait_events(e, [ev for ev in evs if ev[0] != e])

    def emit(self, stack):
        nc = self.nc
        ords = {}
        nsem = {}
        for e in ENGS:
            o = 0
            lst = []
            for op in self.ops[e]:
                if op['inc']:
                    o += 1
                lst.append(o)
            ords[e] = lst
            nsem[e] = (o + EPOCH - 1) // EPOCH
        sems = {e: [stack.enter_context(nc.semaphore(f"s_{e}_{i}")) for i in range(nsem[e])] for e in ENGS}
        dsems = [stack.enter_context(nc.semaphore(f"d_{i}")) for i in range(len(self.dma_cnt))]
        block = stack.enter_context(nc.Block())

        def run(e, eng):
            for i, op in enumerate(self.ops[e]):
                for k, v in op['waits'].items():
                    if isinstance(k, tuple):
                        eng.wait_ge(dsems[k[1]], 16 * self.dma_cnt[k[1]] if v == BIG else v)
                    else:
                        o = ords[k][v - 1]
                        ep = (o - 1) // EPOCH
                        eng.wait_ge(sems[k][ep], o - ep * EPOCH)
                if op['fn'] is None:
                    continue
                inst = op['fn'](eng)
                if op['dma'] is not None:
                    inst.then_inc(dsems[op['dma']], 16)
                elif op['inc']:
                    o = ords[e][i]
                    ep = (o - 1) // EPOCH
                    inst.then_inc(sems[e][ep], 1)

        @block.tensor
        def _(eng):
            run('pe', eng)

        @block.scalar
        def _(eng):
            run('act', eng)

        @block.vector
        def _(eng):
            run('dve', eng)

        @block.gpsimd
        def _(eng):
            run('pool', eng)

        @block.sync
        def _(eng):
            run('sp', eng)


class Arena:
    def __init__(self, ap, nbytes):
        self.ap = ap
        self.nbytes = nbytes
        self.off = 0

    def reset(self, off=0):
        self.off = off

    def alloc(self, free_shape, dt):
        esz = 4 if dt in (F32, I32) else 2
        n = 1
        for s in free_shape:
            n *= s
        nb = n * esz
        self.off = (self.off + 31) // 32 * 32
        assert self.off + nb <= self.nbytes, f"arena overflow {self.off}+{nb}>{self.nbytes}"
        v = self.ap[:, self.off // 2:(self.off + nb) // 2]
        if esz == 4:
            v = v.bitcast(dt)
        self.off += nb
        if len(free_shape) == 2:
            v = v.rearrange("p (a b) -> p a b", b=free_shape[1])
        elif len(free_shape) == 3:
            v = v.rearrange("p (a b c) -> p a b c", b=free_shape[1], c=free_shape[2])
        return v


class Ring:
    def __init__(self, slots):
        self.slots = slots
        self.n = len(slots)
        self.next = 0


class Stream:
    def __init__(self, P, rings, sems):
        self.P = P
        self.rings = rings
        self.sems = sems
        self.specs = []
        self.cur = 0
        self.issued = 0
        self.inflight = {k: 0 for k in rings}
        self.slot_of = {}

    def add(self, ring, key, fns):
        self.specs.append((ring, key, fns))

    def _issue(self):
        HORIZON = 48
        if not hasattr(self, 'scan'):
            self.scan = 0
        while self.scan < len(self.specs) and self.scan in self.slot_of:
            self.scan += 1
        blocked = set()
        i = self.scan
        end = min(len(self.specs), self.cur + HORIZON)
        n_t0 = getattr(self, 'n_tile0', 0)
        while i < end and len(blocked) < len(self.rings):
            if i in self.slot_of:
                i += 1
                continue
            ring, key, fns = self.specs[i]
            R = self.rings[ring]
            if ring in blocked:
                i += 1
                continue
            if self.inflight[ring] >= R.n or (i < n_t0 and key not in self.cast_sem):
                blocked.add(ring)
                i += 1
                continue
            s = R.next
            R.next = (R.next + 1) % R.n
            ap, res = R.slots[s]
            if i < n_t0:
                self.P.wait_events('sp', [(('d', self.cast_sem[key]), BIG)])
            for j, f in enumerate(fns):
                fn = f(ap)
                if j == 0:
                    self.P.op('sp', fn, writes=[res], dma_sem=self.sems[ring][s])
                else:
                    ev = self.P.op('sp', fn, writes=[], dma_sem=self.sems[ring][s])
                    res.w = ev
            self.slot_of[i] = s
            self.inflight[ring] += 1
            i += 1

    def get(self, key):
        assert self.cur < len(self.specs), f"stream exhausted at {key}"
        ring, k, _ = self.specs[self.cur]
        assert k == key, f"stream order mismatch: expected {k}, got {key}"
        self._issue()
        assert self.cur in self.slot_of, f"item {key} could not be issued (cast not pumped or ring full)"
        s = self.slot_of[self.cur]
        self.cur += 1
        return self.rings[ring].slots[s]

    def release(self, ring):
        self.inflight[ring] -= 1
        self._issue()


class _Stop(Exception):
    pass


def build_program(S, kstop=None):
    def ckpt(name):
        if kstop == name:
            raise _Stop()
    NT = S // T
    NBLK = S // 128
    nc = bass.Bass("TRN2", target_bir_lowering=False)

    def din(name, shape, dt=F32):
        return nc.dram_tensor(name, shape, dt, kind="ExternalInput").ap()
    x_d = din("x", [S, D])
    mem_d = din("mem", [256, D])
    pos_d = din("pos", [128, NBLK], I32)
    w_in_d = din("w_in", [D, NIN])
    w_kv_d = din("w_kv", [D, 2048])
    w_ml_d = din("w_ml", [1536, D])
    w_rt_d = din("w_rt", [1536, D])
    w_xa_d = din("w_xa", [1024, D])
    w_out_d = din("w_out", [D, D])
    cols_d = din("cols", [128, NCOLS])
    rows_d = din("rows", [1, NTM])
    fg_d = din("fg", [1, D])
    identb_d = din("identb", [128, 128], BF16)
    identf_d = din("identf", [128, 128])
    maskT_d = din("maskT", [128, 128])
    intraT_d = din("intraT", [128, 768])
    qdec_d = din("qdec", [128, 768])
    freq_d = din("freq", [128, 64])
    out_d = nc.dram_tensor("out", [S, D], F32, kind="ExternalOutput").ap()
    s_fm = nc.dram_tensor("s_fm", [NFB, 128, KC, 128], BF16).ap()
    s_tm = nc.dram_tensor("s_tm", [128, KC, NTM], BF16).ap()
    s_ml = nc.dram_tensor("s_ml", [16, 128, 12, 128], BF16).ap()
    s_rt = nc.dram_tensor("s_rt", [16, 128, 12, 128], BF16).ap()
    s_xa = nc.dram_tensor("s_xa", [16, 128, 8, 128], BF16).ap()
    s_out = nc.dram_tensor("s_out", [128, KC, D], BF16).ap()

    P = Prog(nc)
    st = contextlib.ExitStack()
    with st:
        def sb(name, shape, dt):
            return st.enter_context(nc.sbuf_tensor(name, shape, dt))
        hT = sb("hT", [128, KC, T], BF16)
        mlo = sb("mlo", [128, 12, T], BF16)
        rto = sb("rto", [128, 12, T], BF16)
        xao = sb("xao", [128, 8, T], BF16)
        NWFM, NWTM = 5, 3
        wfm_t = sb("wfm", [128, NWFM, KC, 128], BF16)
        wtm_t = sb("wtm", [128, NWTM, KC, 256], BF16)
        brow_t = sb("brow", [128, 2, 256], F32)
        xt_t = sb("xt", [128, 2, D], F32)
        junk = sb("junk", [128, D], BF16)
        junk2 = sb("junk2", [128, 512], BF16)
        fgbc = sb("fgbc", [128, D], F32)
        cols = sb("cols_sb", [128, NCOLS], F32)
        identb = sb("identb_sb", [128, 128], BF16)
        identf = sb("identf_sb", [128, 128], F32)
        maskT = sb("maskT_sb", [128, 128], F32)
        intraT = sb("intraT_sb", [128, 6, 128], F32)
        qdec = sb("qdec_sb", [128, 6, 128], F32)
        freq = sb("freq_sb", [128, 64], F32)
        posi = sb("posi", [128, NBLK], I32)
        posf = sb("posf", [128, NBLK], F32)
        wif = sb("wif", [128, KC, 12], BF16)
        mkT = sb("mkT", [128, 8, 256], BF16)
        mvv = sb("mvv", [128, 2, 1024], BF16)
        Cst = sb("Cst", [128, 6, 257], F32)
        Cb = sb("Cb", [128, 257], BF16)
        Rst = sb("Rst", [128, 6, 256], F32)
        Rb = sb("Rb", [128, 256], BF16)
        carry = sb("carry", [128, 12, 3], F32)
        mstate = sb("mstate", [128, 1], F32)
        mhalf = sb("mhalf", [128, 1], F32)
        ones6 = sb("ones6", [128, 128], F32)
        cosT = sb("cosT", [128, NB, 64], F32)
        sinT = sb("sinT", [128, NB, 64], F32)
        stat = sb("stat", [128, 64], F32)
        ARENA_BYTES = 50 * 1024
        arena_t = sb("arena", [128, ARENA_BYTES // 2], BF16)
        AR = Arena(arena_t, ARENA_BYTES)
        psum = [st.enter_context(nc.psum_tensor(f"ps{i}", [128, 512], F32)) for i in range(8)]
        r_ps = [Res(f"ps{i}") for i in range(8)]
        bank_ctr = [0]

        def bank():
            b = bank_ctr[0] % 8
            bank_ctr[0] += 1
            return psum[b], r_ps[b]

        R = {k: Res(k) for k in ['hT', 'mlo', 'rto', 'xao', 'junk', 'fgbc', 'cols', 'identb', 'identf', 'maskT', 'intraT',
                                 'qdec', 'freq', 'posi', 'posf', 'wif', 'mkT', 'mvv', 'Cb', 'Rb', 'carry', 'mstate',
                                 'mhalf', 'ones6', 'cosT', 'sinT', 'stat_a', 'stat_h0', 'stat_h1', 'stat_x0', 'stat_x1', 'junk0', 'junk1', 'scratch']}
        r_C = [Res(f"C{h}") for h in range(6)]
        r_R = [Res(f"R{h}") for h in range(6)]
        r_xt = [Res("xt0"), Res("xt1")]
        r_brow = [Res("br0"), Res("br1")]
        r_hTb = [Res(f"hTb{b}") for b in range(NB)]
        A = {}

        def ares(name):
            if name not in A:
                A[name] = Res(name)
            return A[name]

        s_const = P.new_dma_sem(total_only=True)
        s_xt = [P.new_dma_sem(), P.new_dma_sem()]
        s_brow = [P.new_dma_sem(), P.new_dma_sem()]
        s_pre = P.new_dma_sem()
        s_y = [P.new_dma_sem(), P.new_dma_sem()]
        r_y = [Res('y0'), Res('y1')]
        rings = {
            'wfm': Ring([(wfm_t[:, i], Res(f"wfm{i}")) for i in range(NWFM)]),
            'wtm': Ring([(wtm_t[:, i], Res(f"wtm{i}")) for i in range(NWTM)]),
        }
        ring_sems = {k: [P.new_dma_sem() for _ in range(r.n)] for k, r in rings.items()}

        def cload(dst, src, res):
            P.op('sp', CALL('dma_start', out=dst, in_=src), writes=[res], dma_sem=s_const)
        cload(cols[:], cols_d, R['cols'])
        cload(identb[:], identb_d, R['identb'])
        cload(identf[:], identf_d, R['identf'])
        cload(maskT[:], maskT_d, R['maskT'])
        cload(intraT[:], intraT_d.rearrange("p (h l) -> p h l", l=128), R['intraT'])
        cload(qdec[:], qdec_d.rearrange("p (h l) -> p h l", l=128), R['qdec'])
        cload(freq[:], freq_d, R['freq'])
        cload(posi[:], pos_d, R['posi'])
        cload(fgbc[:], fg_d[0:1, :].to_broadcast([128, D]), R['fgbc'])
        P.op('pool', CALL('dma_start', out=wif[:], in_=w_in_d[:, O_MLI:O_MLI + 12].rearrange("(kc p) n -> p kc n", p=128)),
             writes=[R['wif']], dma_sem=s_pre)
        P.op('dve', CALL('memset', mhalf[:], -0.5), writes=[R['mhalf']])
        P.op('dve', CALL('memset', ones6[:], 1.0), writes=[R['ones6']])
        P.op('dve', CALL('memset', mstate[:], 0.0), writes=[R['mstate']])
        P.op('dve', CALL('memset', carry[:], 0.0), writes=[R['carry']])
        P.op('dve', CALL('memset', Cst[:], 0.0), writes=r_C)
        P.op('dve', CALL('memset', Rst[:], 0.0), writes=r_R)
        P.op('dve', CALL('tensor_scalar', out=cols[:, C_HBFM:C_HBFM + NFB], in0=cols[:, C_BFM:C_BFM + NFB], scalar1=0.5, scalar2=None, op0=ALU.mult),
             reads=[R['cols']], writes=[R['cols']])
        P.op('dve', CALL('tensor_scalar', out=cols[:, C_MLG:C_MLG + 12], in0=cols[:, C_MLG:C_MLG + 12], scalar1=0.5, scalar2=None, op0=ALU.mult),
             reads=[R['cols']], writes=[R['cols']])
        P.op('dve', CALL('tensor_scalar', out=cols[:, C_BF:C_BF + 1], in0=cols[:, C_BF:C_BF + 1], scalar1=-1.0, scalar2=None, op0=ALU.mult),
             reads=[R['cols']], writes=[R['cols']])
        P.op('dve', CALL('tensor_copy', out=posf[:], in_=posi[:]), reads=[R['posi']], writes=[R['posf']])

        xt_ctr = [0]

        def load_rows(src_ap):
            s = xt_ctr[0] % 2
            xt_ctr[0] += 1
            ap = xt_t[:, s]
            P.op('sp', CALL('dma_start', out=ap, in_=src_ap), writes=[r_xt[s]], dma_sem=s_xt[s])
            return ap, r_xt[s]

        def rms_scale(xs, rx, scol):
            ss = stat[:, scol:scol + 1]
            rs = stat[:, scol + 1:scol + 2]
            P.op('act', CALL('activation', out=junk[:], in_=xs, func=AF.Square, accum_out=ss), reads=[rx], writes=[R['junk'], R['stat_a']])
            P.op('dve', CALL('tensor_scalar', out=rs, in0=ss, scalar1=1.0 / D, scalar2=EPS, op0=ALU.mult, op1=ALU.add), reads=[R['stat_a']], writes=[R['stat_a']])
            P.op('pool', CALL('tensor_tensor', out=rs, in0=rs, in1=mhalf[:], op=ALU.pow), reads=[R['stat_a'], R['mhalf']], writes=[R['stat_a']])
            P.op('dve', CALL('tensor_scalar', out=xs, in0=xs, scalar1=rs, scalar2=None, op0=ALU.mult), reads=[rx, R['stat_a']], writes=[rx])

        def transpose_rows(xs, rx, dst_fn, gbase, rdst):
            for q in range(4):
                pb, rpb = bank()
                for j in range(4):
                    kc = q * 4 + j
                    P.op('pe', CALL('transpose', out=pb[:, j * 128:(j + 1) * 128], in_=xs[:, kc * 128:(kc + 1) * 128], identity=identf[:]),
                         reads=[rx, R['identf']], writes=[rpb])
                eng = 'dve' if q % 2 == 0 else 'pool'
                eng = 'dve'
                P.op(eng, CALL('tensor_tensor', out=dst_fn(q * 4), in0=pb[:].rearrange("p (a b) -> p a b", b=128),
                                                                 in1=cols[:, gbase + q * 4:gbase + q * 4 + 4].unsqueeze(2).to_broadcast([128, 4, 128]), op=ALU.mult),
                     reads=[rpb, R['cols']], writes=[rdst])

        def mm_fm(wap, rw, rhs_fn, r_rhs, nk, n=T, m=128):
            pb, rpb = bank()
            for kc in range(nk):
                P.op('pe', CALL('matmul', pb[0:m, 0:n], lhsT=wap[:, kc, 0:m], rhs=rhs_fn(kc), start=(kc == 0), stop=(kc == nk - 1)),
                     reads=[rw] + r_rhs, writes=[rpb])
            return pb, rpb

        SM = Stream(P, rings, ring_sems)

        def fm_spec(b):
            SM.add('wfm', ('fm', b), [lambda ap, b=b: (CALL('dma_start', out=ap, in_=s_fm[b]))])

        def tm_spec(c0, tag):
            SM.add('wtm', ('tm', tag, c0), [lambda ap, c0=c0: (CALL('dma_start', out=ap, in_=s_tm[:, :, c0:c0 + 256]))])

        def br_spec(which, n):
            src = {'ml': s_ml, 'rt': s_rt, 'xa': s_xa}[which]
            nk = 8 if which == 'xa' else 12
            SM.add('wfm', ('br', which, n), [lambda ap, n=n, nk=nk, src=src: (CALL('dma_start', out=ap[:, 0:nk, :], in_=src[n]))])

        def out_spec(q):
            SM.add('wtm', ('out', q), [lambda ap, q=q: (CALL('dma_start', out=ap, in_=s_out[:, :, q * 256:(q + 1) * 256]))])

        for tt in range(NT):
            for h in range(6):
                fm_spec(FB_MLQ + h); fm_spec(FB_MLK + h)
                tm_spec(h * 256, 'mlv')
                for half in range(2):
                    fm_spec(FB_MLO + 2 * h + half); fm_spec(FB_MLZ + 2 * h + half)
            for h in range(6):
                tm_spec(1536 + h * 256, 'rtqk')
                tm_spec(3072 + h * 256, 'rtv')
                for half in range(2):
                    fm_spec(FB_RTZ + 2 * h + half)
            for h in range(4):
                for half in range(2):
                    fm_spec(FB_XAQ + 2 * h + half)
                for half in range(2):
                    fm_spec(FB_XAZ + 2 * h + half)
            for n in range(16):
                for gi, which in enumerate(['ml', 'rt', 'xa']):
                    br_spec(which, n)
                    fm_spec(FB_G + gi * 16 + n)
            for pair in range(NB // 2):
                for q in range(8):
                    out_spec(q)

        def kc_view(ap):
            return ap.rearrange("(kc p) n -> p kc n", p=128)

        def cast_pairs(key):
            if key[0] == 'fm':
                b_ = key[1]
                return [(s_fm[b_], kc_view(w_in_d[:, FM_OFFS[b_]:FM_OFFS[b_] + 128]))]
            if key[0] == 'tm':
                tag, c0 = key[1], key[2]
                if tag == 'mlv':
                    return [(s_tm[:, :, c0:c0 + 256], kc_view(w_in_d[:, O_MLV + c0:O_MLV + c0 + 256]))]
                if tag == 'rtqk':
                    h_ = (c0 - 1536) // 256
                    return [(s_tm[:, :, c0:c0 + 128], kc_view(w_in_d[:, O_RTQ + h_ * 128:O_RTQ + (h_ + 1) * 128])),
                            (s_tm[:, :, c0 + 128:c0 + 256], kc_view(w_in_d[:, O_RTK + h_ * 128:O_RTK + (h_ + 1) * 128]))]
                off = c0 - 3072
                return [(s_tm[:, :, c0:c0 + 256], kc_view(w_in_d[:, O_RTV + off:O_RTV + off + 256]))]
            if key[0] == 'br':
                which, n_ = key[1], key[2]
                dsts = {'ml': s_ml, 'rt': s_rt, 'xa': s_xa}[which]
                srcs_ = {'ml': w_ml_d, 'rt': w_rt_d, 'xa': w_xa_d}[which]
                return [(dsts[n_], kc_view(srcs_[:, n_ * 128:(n_ + 1) * 128]))]
            q_ = key[1]
            return [(s_out[:, :, q_ * 256:(q_ + 1) * 256], kc_view(w_out_d[:, q_ * 256:(q_ + 1) * 256]))]

        n_tile0 = len(SM.specs) // NT
        cast_pending = []
        _seen_keys = set()
        for (_, key_, _) in SM.specs[:n_tile0]:
            if key_ not in _seen_keys:
                _seen_keys.add(key_)
                cast_pending.append(key_)
        cast_sem = {}

        def pump(n_items):
            done_ = 0
            while done_ < n_items and cast_pending:
                sem_ = P.new_dma_sem(total_only=True)
                for _ in range(min(3, n_items - done_, len(cast_pending))):
                    key_ = cast_pending.pop(0)
                    for (dst_, src_) in cast_pairs(key_):
                        P.op('pool', CALL('dma_start', out=dst_, in_=src_), writes=[], dma_sem=sem_)
                    cast_sem[key_] = sem_
                    done_ += 1
        SM.n_tile0 = n_tile0
        SM.cast_sem = cast_sem
        pump(36)

        def getw(key):
            ap, res = SM.get(key)
            return ap, res

        def relw(ring):
            SM.release(ring)

        brow_ctr = [0]

        def load_brow(c0):
            s = brow_ctr[0] % 2
            brow_ctr[0] += 1
            ap = brow_t[:, s]
            P.op('sp', CALL('dma_start', out=ap, in_=rows_d[0:1, c0:c0 + 256].to_broadcast([128, 256])), writes=[r_brow[s]], dma_sem=s_brow[s])
            return ap, r_brow[s]

        last_store = {}

        def body():
            ckpt('const')
            AR.reset()
            A.clear()
            mnT = AR.alloc([KC, 256], BF16)
            wkv = AR.alloc([2, KC, 256], BF16)
            r_wkv = [Res("wkv0"), Res("wkv1")]
            s_wkv = [P.new_dma_sem(), P.new_dma_sem()]
            for mb in range(2):
                xs, rx = load_rows(mem_d[mb * 128:(mb + 1) * 128, :])
                rms_scale(xs, rx, 0)
                transpose_rows(xs, rx, lambda kc0, mb=mb: mnT[:, kc0:kc0 + 4, mb * 128:(mb + 1) * 128], C_MGCOL, ares('mnT'))
            for piece in range(8):
                s = piece % 2
                P.op('pool', CALL('dma_start', out=wkv[:, s], in_=w_kv_d[:, piece * 256:(piece + 1) * 256].rearrange("(kc p) n -> p kc n", p=128)),
                     writes=[r_wkv[s]], dma_sem=s_wkv[s])
                if piece < 4:
                    for j in range(2):
                        pb, rpb = bank()
                        for kc in range(KC):
                            P.op('pe', CALL('matmul', pb[:, 0:256], lhsT=wkv[:, s, kc, j * 128:(j + 1) * 128], rhs=mnT[:, kc, :], start=(kc == 0), stop=(kc == KC - 1)),
                                 reads=[r_wkv[s], ares('mnT')], writes=[rpb])
                        P.op('act', CALL('copy', out=mkT[:, piece * 2 + j, :], in_=pb[:, 0:256]), reads=[rpb], writes=[R['mkT']])
                else:
                    for mb in range(2):
                        pb, rpb = bank()
                        for kc in range(KC):
                            P.op('pe', CALL('matmul', pb[:, 0:256], lhsT=mnT[:, kc, mb * 128:(mb + 1) * 128], rhs=wkv[:, s, kc, :], start=(kc == 0), stop=(kc == KC - 1)),
                                 reads=[r_wkv[s], ares('mnT')], writes=[rpb])
                        P.op('act', CALL('copy', out=mvv[:, mb, (piece - 4) * 256:(piece - 3) * 256], in_=pb[:, 0:256]), reads=[rpb], writes=[R['mvv']])
            P.barrier()

            LNSC = math.log(128.0 ** -0.5)

            ckpt('pre')
            for tt in range(NT):
                tok0 = tt * T
                def phaseA(t_):
                    RA = dict(ang=Res('a_ang'), kf=Res('a_kf'), cang=Res('a_cang'))
                    tk0 = t_ * T
                    for blk in range(NB):
                        xs, rx = load_rows(x_d[tk0 + blk * 128:tk0 + (blk + 1) * 128, :])
                        rms_scale(xs, rx, 2 * (blk % 2))
                        yield
                        transpose_rows(xs, rx, lambda kc0, blk=blk: hT[:, kc0:kc0 + 4, blk * 128:(blk + 1) * 128], C_GCOL, R['hT'])
                        yield
                    AR.reset(OFF_ET)
                    ang = AR.alloc([NB, 64], F32)
                    kf = AR.alloc([NB, 64], F32)
                    cang = AR.alloc([NB, 64], F32)
                    ki = cang.bitcast(I32)
                    P.op('dve', CALL('tensor_tensor', out=ang, in0=posf[:, t_ * NB:(t_ + 1) * NB].unsqueeze(2).to_broadcast([128, NB, 64]),
                                                          in1=freq[:].unsqueeze(1).to_broadcast([128, NB, 64]), op=ALU.mult),
                         reads=[R['posf'], R['freq']], writes=[RA['ang']])
                    P.op('dve', CALL('tensor_scalar', out=kf, in0=ang, scalar1=1.0 / (2 * PI), scalar2=None, op0=ALU.mult), reads=[RA['ang']], writes=[RA['kf']])
                    P.op('dve', CALL('tensor_copy', out=ki, in_=kf), reads=[RA['kf']], writes=[RA['cang']])
                    P.op('dve', CALL('tensor_copy', out=kf, in_=ki), reads=[RA['cang']], writes=[RA['kf']])
                    P.op('dve', CALL('scalar_tensor_tensor', out=ang, in0=kf, scalar=-TWO_PI_HI, in1=ang, op0=ALU.mult, op1=ALU.add), reads=[RA['kf'], RA['ang']], writes=[RA['ang']])
                    P.op('dve', CALL('scalar_tensor_tensor', out=ang, in0=kf, scalar=-TWO_PI_LO, in1=ang, op0=ALU.mult, op1=ALU.add), reads=[RA['kf'], RA['ang']], writes=[RA['ang']])
                    P.op('dve', CALL('tensor_scalar', out=cang, in0=ang, scalar1=PI / 2, scalar2=None, op0=ALU.add), reads=[RA['ang']], writes=[RA['cang']])
                    P.op('dve', CALL('tensor_scalar', out=kf, in0=cang, scalar1=PI, scalar2=-2 * PI, op0=ALU.is_gt, op1=ALU.mult), reads=[RA['cang']], writes=[RA['kf']])
                    P.op('dve', CALL('tensor_tensor', out=cang, in0=cang, in1=kf, op=ALU.add), reads=[RA['cang'], RA['kf']], writes=[RA['cang']])
                    P.op('dve', CALL('tensor_scalar', out=ang, in0=ang, scalar1=PI_CLAMP, scalar2=-PI_CLAMP, op0=ALU.min, op1=ALU.max), reads=[RA['ang']], writes=[RA['ang']])
                    P.op('dve', CALL('tensor_scalar', out=cang, in0=cang, scalar1=PI_CLAMP, scalar2=-PI_CLAMP, op0=ALU.min, op1=ALU.max), reads=[RA['cang']], writes=[RA['cang']])
                    P.op('act', CALL('activation', out=sinT[:], in_=ang, func=AF.Sin), reads=[RA['ang']], writes=[R['sinT']])
                    P.op('act', CALL('activation', out=cosT[:], in_=cang, func=AF.Sin), reads=[RA['cang']], writes=[R['cosT']])

                    yield

                if tt == 0:
                    for _ in phaseA(0):
                        pass
                A.clear()
                AR.reset(0)
                ckpt('rope')
                g_i = AR.alloc([T], F32)
                g_e = AR.alloc([T], F32)
                g_nb = AR.alloc([T], F32)
                g_a = AR.alloc([T], F32)
                g_w = AR.alloc([T], F32)
                g_fl = AR.alloc([T], F32)
                g_sm = AR.alloc([32], F32)
                Dg = AR.alloc([6, NB], F32)
                gateT = AR.alloc([NB, 16], F32)
                decbc = AR.alloc([6, NB], F32)
                rG = ares('gates')
                pbi, rpbi = mm_fm(wif[:, :, 0:6], R['wif'], lambda kc: hT[:, kc, :], [R['hT']], KC, m=6)
                pbf, rpbf = mm_fm(wif[:, :, 6:12], R['wif'], lambda kc: hT[:, kc, :], [R['hT']], KC, m=6)
                P.op('act', CALL('activation', out=g_i[0:6], in_=pbi[0:6, :], func=AF.Identity, bias=cols[0:6, C_BI:C_BI + 1]), reads=[rpbi, R['cols']], writes=[rG])
                P.op('act', CALL('activation', out=g_e[0:6], in_=pbf[0:6, :], func=AF.Exp, scale=-1.0, bias=cols[0:6, C_BF:C_BF + 1]), reads=[rpbf, R['cols']], writes=[rG])
                P.op('act', CALL('activation', out=g_e[0:6], in_=g_e[0:6], func=AF.Ln, bias=1.0), reads=[rG], writes=[rG])
                for c in range(NB):
                    P.op('dve', CALL('tensor_tensor_scan', out=g_nb[0:6, c * 128:(c + 1) * 128], data0=ones6[0:6, :], data1=g_e[0:6, c * 128:(c + 1) * 128],
                                                                  initial=0.0, op0=ALU.mult, op1=ALU.add), reads=[rG, R['ones6']], writes=[rG])
                P.op('dve', CALL('tensor_tensor', out=g_a[0:6], in0=g_i[0:6], in1=g_nb[0:6], op=ALU.add), reads=[rG], writes=[rG])
                P.op('dve', CALL('tensor_reduce', out=g_sm[0:6, 0:4], in_=g_a[0:6].rearrange("p (c n) -> p c n", n=128), axis=AX.X, op=ALU.max), reads=[rG], writes=[rG])
                P.op('dve', CALL('tensor_scalar', out=g_sm[0:6, 4:8], in0=g_nb[0:6].rearrange("p (c n) -> p c n", n=128)[:, :, 127], scalar1=-1.0, scalar2=None, op0=ALU.mult), reads=[rG], writes=[rG])
                P.op('dve', CALL('tensor_tensor_scan', out=g_sm[0:6, 8:12], data0=g_sm[0:6, 0:4], data1=g_sm[0:6, 4:8], initial=mstate[0:6, 0:1], op0=ALU.max, op1=ALU.add),
                     reads=[rG, R['mstate']], writes=[rG])
                P.op('dve', CALL('tensor_copy', out=g_sm[0:6, 12:13], in_=mstate[0:6, 0:1]), reads=[R['mstate'], rG], writes=[rG])
                P.op('dve', CALL('tensor_copy', out=g_sm[0:6, 13:16], in_=g_sm[0:6, 8:11]), reads=[rG], writes=[rG])
                P.op('dve', CALL('tensor_tensor', out=g_sm[0:6, 16:20], in0=g_sm[0:6, 12:16], in1=g_sm[0:6, 0:4], op=ALU.max), reads=[rG], writes=[rG])
                P.op('dve', CALL('tensor_copy', out=mstate[0:6, 0:1], in_=g_sm[0:6, 11:12]), reads=[rG], writes=[R['mstate']])
                P.op('dve', CALL('tensor_tensor', out=g_sm[0:6, 20:24], in0=g_sm[0:6, 12:16], in1=g_sm[0:6, 16:20], op=ALU.subtract), reads=[rG], writes=[rG])
                P.op('act', CALL('activation', out=g_sm[0:6, 24:28], in_=g_sm[0:6, 20:24], func=AF.Exp), reads=[rG], writes=[rG])
                P.op('dve', CALL('tensor_tensor', out=g_w[0:6].rearrange("p (c n) -> p c n", n=128), in0=g_a[0:6].rearrange("p (c n) -> p c n", n=128),
                                                      in1=g_sm[0:6, 16:20].unsqueeze(2).to_broadcast([6, NB, 128]), op=ALU.subtract), reads=[rG], writes=[rG])
                P.op('dve', CALL('tensor_scalar', out=g_w[0:6], in0=g_w[0:6], scalar1=LNSC, scalar2=None, op0=ALU.add), reads=[rG], writes=[rG])
                P.op('act', CALL('activation', out=g_w[0:6], in_=g_w[0:6], func=AF.Exp), reads=[rG], writes=[rG])
                P.op('dve', CALL('tensor_tensor', out=g_fl[0:6].rearrange("p (c n) -> p c n", n=128), in0=g_nb[0:6].rearrange("p (c n) -> p c n", n=128),
                                                      in1=g_sm[0:6, 16:20].unsqueeze(2).to_broadcast([6, NB, 128]), op=ALU.subtract), reads=[rG], writes=[rG])
                P.op('act', CALL('activation', out=g_fl[0:6], in_=g_fl[0:6], func=AF.Exp), reads=[rG], writes=[rG])
                pg, rpg = bank()
                for c in range(NB):
                    P.op('pe', CALL('transpose', out=pg[:, c * 16:c * 16 + 6], in_=g_w[0:6, c * 128:(c + 1) * 128], identity=identf[0:6, 0:6]), reads=[rG, R['identf']], writes=[rpg])
                    P.op('pe', CALL('transpose', out=pg[:, c * 16 + 8:c * 16 + 14], in_=g_fl[0:6, c * 128:(c + 1) * 128], identity=identf[0:6, 0:6]), reads=[rG, R['identf']], writes=[rpg])
                rGT = ares('gateT')
                P.op('act', CALL('copy', out=gateT.rearrange("p c (t s) -> p c t s", s=8)[:, :, :, 0:6], in_=pg[:, 0:NB * 16].rearrange("p (c t s) -> p c t s", t=2, s=8)[:, :, :, 0:6]),
                     reads=[rpg], writes=[rGT])
                P.op('dve', CALL('tensor_tensor', out=Dg[0:6], in0=g_sm[0:6, 24:28].unsqueeze(1).to_broadcast([6, 6, NB]),
                                                      in1=identf[0:6, 0:6].unsqueeze(2).to_broadcast([6, 6, NB]), op=ALU.mult), reads=[rG, R['identf']], writes=[ares('Dg')])
                pd, rpd = bank()
                P.op('pe', CALL('matmul', pd[:, 0:6 * NB], lhsT=ones6[0:6, :], rhs=Dg[0:6].rearrange("p a b -> p (a b)"), start=True, stop=True), reads=[ares('Dg'), R['ones6']], writes=[rpd])
                P.op('act', CALL('copy', out=decbc.rearrange("p a b -> p (a b)"), in_=pd[:, 0:6 * NB]), reads=[rpd], writes=[rGT])

                head_base = AR.off
                assert head_base <= OFF_PT, head_base

                def interleave(a, b):
                    da = a is None
                    db = b is None
                    while not (da and db):
                        if not da:
                            try:
                                next(a)
                            except StopIteration:
                                da = True
                        if not db:
                            try:
                                next(b)
                            except StopIteration:
                                db = True

                def drive(pairs):
                    for _ in pairs[0][0]():
                        pass
                    for i in range(len(pairs)):
                        a = pairs[i][1]()
                        b = pairs[i + 1][0]() if i + 1 < len(pairs) else None
                        interleave(a, b)

                def norm_a(pn, rpn, cp, tmp):
                    numsb = tmp['numsb'][cp]
                    sm = stat[:, 8 + 16 * cp:24 + 16 * cp]
                    rS = R['stat_h%d' % cp]
                    rn = ares('numsb%d' % cp)
                    P.op('act', CALL('activation', out=numsb, in_=pn[:, 0:256], func=AF.Identity, accum_out=sm[:, 0:1]),
                         reads=[rpn], writes=[rn, rS])
                    P.op('act', CALL('activation', out=junk2[:, 256 * cp:256 * cp + 256], in_=numsb, func=AF.Square, accum_out=sm[:, 1:2]), reads=[rn], writes=[R['junk%d' % cp], rS])

                def norm_b(ddcol, cp, tmp):
                    numsb, hn = tmp['numsb'][cp], tmp['hn'][cp]
                    sm = stat[:, 8 + 16 * cp:24 + 16 * cp]
                    rS = R['stat_h%d' % cp]
                    rn = ares('numsb%d' % cp)
                    P.op('dve', CALL('scalar_tensor_tensor', out=sm[:, 2:3], in0=sm[:, 0:1], scalar=-1.0 / 65536, in1=sm[:, 0:1], op0=ALU.mult, op1=ALU.mult), reads=[rS], writes=[rS])
                    if ddcol is not None:
                        P.op('dve', CALL('scalar_tensor_tensor', out=sm[:, 3:4], in0=ddcol, scalar=EPS, in1=ddcol, op0=ALU.mult, op1=ALU.mult), reads=[rS], writes=[rS])
                        P.op('dve', CALL('tensor_tensor', out=sm[:, 2:3], in0=sm[:, 2:3], in1=sm[:, 3:4], op=ALU.add), reads=[rS], writes=[rS])
                    else:
                        P.op('dve', CALL('tensor_scalar', out=sm[:, 2:3], in0=sm[:, 2:3], scalar1=EPS, scalar2=None, op0=ALU.add), reads=[rS], writes=[rS])
                    P.op('dve', CALL('scalar_tensor_tensor', out=sm[:, 4:5], in0=sm[:, 1:2], scalar=1.0 / 256, in1=sm[:, 2:3], op0=ALU.mult, op1=ALU.add), reads=[rS], writes=[rS])
                    P.op('pool', CALL('tensor_tensor', out=sm[:, 6:7], in0=sm[:, 4:5], in1=mhalf[:], op=ALU.pow), reads=[rS, R['mhalf']], writes=[rS])
                    P.op('dve', CALL('scalar_tensor_tensor', out=sm[:, 7:8], in0=sm[:, 0:1], scalar=-1.0 / 256, in1=sm[:, 6:7], op0=ALU.mult, op1=ALU.mult), reads=[rS], writes=[rS])
                    P.op('act', CALL('activation', out=hn, in_=numsb, func=AF.Identity, scale=sm[:, 6:7], bias=sm[:, 7:8]), reads=[rn, rS], writes=[ares('hn%d' % cp)])

                def norm_tail(cp, tmp, dst, rdst, gcol0, Gt, rGt, c):
                    hn = tmp['hn'][cp]
                    pt, rpt = bank()
                    ptb = pt[:].bitcast(BF16)
                    for half in range(2):
                        P.op('pe', CALL('transpose', out=ptb[:, half * 128:(half + 1) * 128], in_=hn[:, half * 128:(half + 1) * 128], identity=identb[:]),
                             reads=[ares('hn%d' % cp), R['identb']], writes=[rpt])
                    for half in range(2):
                        P.op('dve', CALL('scalar_tensor_tensor', out=dst[:, half, c * 128:(c + 1) * 128], in0=ptb[:, half * 128:(half + 1) * 128],
                                         scalar=cols[:, gcol0 + half:gcol0 + half + 1], in1=Gt[:, half, c * 128:(c + 1) * 128],
                                         op0=ALU.mult, op1=ALU.mult), reads=[rpt, R['cols'], rGt], writes=[rdst])

                def fm_proj(fb):
                    wap, rw = getw(('fm', fb))
                    pb, rpb = mm_fm(wap, rw, lambda kc: hT[:, kc, :], [R['hT']], KC)
                    relw('wfm')
                    return pb, rpb

                def tm_proj(key, c0, dst_fn, rdst):
                    wap, rw = getw(key)
                    bap, rb = load_brow(c0)
                    for c in range(NB):
                        pb, rpb = bank()
                        for kc in range(KC):
                            P.op('pe', CALL('matmul', pb[:, 0:256], lhsT=hT[:, kc, c * 128:(c + 1) * 128], rhs=wap[:, kc, :], start=(kc == 0), stop=(kc == KC - 1)),
                                 reads=[rw, R['hT']], writes=[rpb])
                        P.op('dve', CALL('tensor_tensor', out=dst_fn(c), in0=pb[:, 0:256], in1=bap, op=ALU.add), reads=[rpb, rb], writes=[rdst])
                        yield
                    relw('wtm')

                ckpt('B')
                AR.reset(OFF_PT)
                c_uq = AR.alloc([T + 3], F32)
                c_uk = AR.alloc([T + 3], F32)
                c_yq = AR.alloc([T], F32)
                c_yk = AR.alloc([T], F32)
                c_tO = AR.alloc([2, T], BF16)
                c_sZ = AR.alloc([2, T], BF16)
                assert AR.off <= OFF_NT, AR.off
                AR.reset(OFF_NT)
                tmp = dict(numsb=[AR.alloc([256], F32) for _ in range(2)], hn=[AR.alloc([256], BF16) for _ in range(2)],
                           sTm=[AR.alloc([128], BF16) for _ in range(2)])
                assert AR.off <= OFF_ET, AR.off
                c_live = []
                for par_ in range(2):
                    AR.reset(OFF_L0 + par_ * LIVE_SZ)
                    c_live.append(dict(qT=AR.alloc([T], BF16), kT=AR.alloc([T], BF16), kw=AR.alloc([NB, 128], BF16), vx=AR.alloc([NB, 257], BF16),
                                       Gt=AR.alloc([2, T], BF16)))
                    assert AR.off <= OFF_L0 + (par_ + 1) * LIVE_SZ, AR.off

                def c_proj(h):
                    par = h % 2
                    if tt == 0:
                        pump(12)
                    L = c_live[par]
                    for qi, (fb, u, y, nm) in enumerate([(FB_MLQ + h, c_uq, c_yq, 'q'), (FB_MLK + h, c_uk, c_yk, 'k')]):
                        dstT = L[nm + 'T']
                        pb, rpb = fm_proj(fb)
                        cb = qi * 6 + h
                        ru = ares('u' + nm)
                        P.op('dve', CALL('tensor_copy', out=u[:, 0:3], in_=carry[:, cb, :]), reads=[R['carry']], writes=[ru])
                        P.op('act', CALL('activation', out=u[:, 3:T + 3], in_=pb[:, 0:T], func=AF.Identity, bias=cols[:, C_BFM + fb:C_BFM + fb + 1]),
                             reads=[rpb, R['cols']], writes=[ru])
                        P.op('dve', CALL('tensor_copy', out=carry[:, cb, :], in_=u[:, T:T + 3]), reads=[ru], writes=[R['carry']])
                        ry = ares('y' + nm)
                        P.op('dve', CALL('tensor_scalar', out=y, in0=u[:, 3:T + 3], scalar1=cols[:, C_CONVW + cb * 4 + 3:C_CONVW + cb * 4 + 4],
                                         scalar2=cols[:, C_CONVB + cb:C_CONVB + cb + 1], op0=ALU.mult, op1=ALU.add), reads=[ru, R['cols']], writes=[ry])
                        for k in range(3):
                            P.op('dve', CALL('scalar_tensor_tensor', out=y, in0=u[:, k:k + T], scalar=cols[:, C_CONVW + cb * 4 + k:C_CONVW + cb * 4 + k + 1],
                                             in1=y, op0=ALU.mult, op1=ALU.add), reads=[ru, R['cols'], ry], writes=[ry])
                        P.op('act', CALL('activation', out=dstT, in_=y, func=AF.Silu), reads=[ry], writes=[ares('%sT%d' % (nm, par))])
                        yield
                    vx = L['vx']
                    P.op('dve', CALL('memset', vx[:, :, 256:257], 1.0), writes=[ares('vx%d' % par)])
                    yield from tm_proj(('tm', 'mlv', h * 256), h * 256, lambda c: vx[:, c, 0:256], ares('vx%d' % par))
                    for half in range(2):
                        fbo = FB_MLO + 2 * h + half
                        pb, rpb = fm_proj(fbo)
                        P.op('act', CALL('activation', out=c_tO[:, half, :], in_=pb[:, 0:T], func=AF.Tanh, scale=0.5, bias=cols[:, C_HBFM + fbo:C_HBFM + fbo + 1]),
                             reads=[rpb, R['cols']], writes=[ares('tO')])
                        yield
                        fbz = FB_MLZ + 2 * h + half
                        pb, rpb = fm_proj(fbz)
                        P.op('act', CALL('activation', out=c_sZ[:, half, :], in_=pb[:, 0:T], func=AF.Silu, bias=cols[:, C_BFM + fbz:C_BFM + fbz + 1]),
                             reads=[rpb, R['cols']], writes=[ares('sZ')])
                        yield
                    P.op('dve', CALL('scalar_tensor_tensor', out=L['Gt'], in0=c_tO, scalar=1.0, in1=c_sZ, op0=ALU.add, op1=ALU.mult), reads=[ares('tO'), ares('sZ')], writes=[ares('Gt%d' % par)])
                    pk, rpk = bank()
                    pkb = pk[:].bitcast(BF16)
                    kT, kw = L['kT'], L['kw']
                    for c in range(NB):
                        P.op('pe', CALL('transpose', out=pkb[:, c * 128:(c + 1) * 128], in_=kT[:, c * 128:(c + 1) * 128], identity=identb[:]), reads=[ares('kT%d' % par), R['identb']], writes=[rpk])
                    for c in range(NB):
                        P.op('act', CALL('activation', out=kw[:, c, :], in_=pkb[:, c * 128:(c + 1) * 128], func=AF.Identity, scale=gateT[:, c, h:h + 1]), reads=[rpk, rGT], writes=[ares('kw%d' % par)])
                    yield

                def c_rec(h):
                    par = h % 2
                    L = c_live[par]
                    qT, kT, kw, vx, Gt = L['qT'], L['kT'], L['kw'], L['vx'], L['Gt']
                    rq, rk, rkw, rvx, rGt_ = ares('qT%d' % par), ares('kT%d' % par), ares('kw%d' % par), ares('vx%d' % par), ares('Gt%d' % par)

                    def front(c):
                        cp = c % 2
                        cs = slice(c * 128, (c + 1) * 128)
                        ps_s, rps = bank()
                        P.op('pe', CALL('matmul', ps_s[:, 0:128], lhsT=kT[:, cs], rhs=qT[:, cs], start=True, stop=True), reads=[rk, rq], writes=[rps])
                        P.op('dve', CALL('scalar_tensor_tensor', out=tmp['sTm'][cp], in0=ps_s[:, 0:128], scalar=gateT[:, c, h:h + 1], in1=maskT[:], op0=ALU.mult, op1=ALU.mult),
                             reads=[rps, rGT, R['maskT']], writes=[ares('sTm%d' % cp)])
                        P.op('act', CALL('activation', out=Cb[:], in_=Cst[:, h, :], func=AF.Identity, scale=decbc[:, h, c:c + 1]), reads=[r_C[h], rGT], writes=[R['Cb']])

                    front(0)
                    yield
                    pending = None
                    for c in range(NB):
                        cp = c % 2
                        cs = slice(c * 128, (c + 1) * 128)
                        sTm = tmp['sTm'][cp]
                        rsT = ares('sTm%d' % cp)
                        pn, rpn = bank()
                        P.op('pe', CALL('matmul', pn[:, 0:257], lhsT=sTm, rhs=vx[:, c, :], start=True, stop=False), reads=[rsT, rvx], writes=[rpn])
                        P.op('pe', CALL('matmul', pn[:, 0:257], lhsT=qT[:, cs], rhs=Cb[:], start=False, stop=True), reads=[rq, R['Cb']], writes=[rpn])
                        pkv, rpkv = bank()
                        P.op('pe', CALL('matmul', pkv[:, 0:257], lhsT=kw[:, c, :], rhs=vx[:, c, :], start=True, stop=True), reads=[rkw, rvx], writes=[rpkv])
                        P.op('dve', CALL('scalar_tensor_tensor', out=Cst[:, h, :], in0=Cst[:, h, :], scalar=decbc[:, h, c:c + 1], in1=pkv[:, 0:257], op0=ALU.mult, op1=ALU.add),
                             reads=[r_C[h], rGT, rpkv], writes=[r_C[h]])
                        dcol = 40 + 2 * cp
                        rS = R['stat_h%d' % cp]
                        P.op('act', CALL('activation', out=stat[:, dcol:dcol + 1], in_=pn[:, 256:257], func=AF.Abs), reads=[rpn], writes=[rS])
                        norm_a(pn, rpn, cp, tmp)
                        if c + 1 < NB:
                            front(c + 1)
                        yield
                        P.op('dve', CALL('tensor_tensor', out=stat[:, dcol:dcol + 1], in0=stat[:, dcol:dcol + 1], in1=gateT[:, c, 8 + h:9 + h], op=ALU.max), reads=[rS, rGT], writes=[rS])
                        norm_b(stat[:, dcol:dcol + 1], cp, tmp)
                        if pending is not None:
                            norm_tail(*pending)
                        pending = (cp, tmp, mlo[:, 2 * h:2 * h + 2, :], R['mlo'], C_MLG + 2 * h, Gt, rGt_, c)
                        yield
                    norm_tail(*pending)
                    yield

                AR.reset(OFF_PT)
                d_tq = AR.alloc([NB, 256], F32)
                d_rp1 = AR.alloc([2, 64], F32)
                d_rp2 = AR.alloc([2, 64], F32)
                d_qkr = AR.alloc([NB, 256], BF16)
                d_qf = AR.alloc([T], F32)
                assert AR.off <= OFF_NT, AR.off
                d_live = []
                for par_ in range(2):
                    AR.reset(OFF_L0 + par_ * LIVE_SZ)
                    d_live.append(dict(kd=AR.alloc([NB, 128], BF16), qT=AR.alloc([T], BF16), kT=AR.alloc([T], BF16), qdT=AR.alloc([T], BF16),
                                       vv=AR.alloc([NB, 256], BF16), sZ=AR.alloc([2, T], BF16)))
                    assert AR.off <= OFF_L0 + (par_ + 1) * LIVE_SZ, AR.off

                def d_proj(h):
                    par = h % 2
                    if tt == 0:
                        pump(12)
                    L = d_live[par]
                    gen = tm_proj(('tm', 'rtqk', 1536 + h * 256), 1536 + h * 256, lambda c: d_tq[:, c, :], ares('tq'))
                    for c in range(NB):
                        next(gen)
                        t4 = d_tq[:, c, :].rearrange("p (a b c) -> p a b c", a=2, b=2)
                        o4 = d_qkr[:, c, :].rearrange("p (a b c) -> p a b c", a=2, b=2)
                        cosb = cosT[:, c, :].unsqueeze(1).to_broadcast([128, 2, 64])
                        sinb = sinT[:, c, :].unsqueeze(1).to_broadcast([128, 2, 64])
                        rr, rr2 = ares('rope'), ares('rope2')
                        P.op('dve', CALL('tensor_tensor', out=d_rp1, in0=t4[:, :, 0, :], in1=cosb, op=ALU.mult), reads=[ares('tq'), R['cosT']], writes=[rr])
                        P.op(ROPE_ENG, CALL('tensor_tensor', out=d_rp2, in0=t4[:, :, 1, :], in1=sinb, op=ALU.mult), reads=[ares('tq'), R['sinT']], writes=[rr2])
                        P.op('dve', CALL('tensor_tensor', out=o4[:, :, 0, :], in0=d_rp1, in1=d_rp2, op=ALU.subtract), reads=[rr, rr2], writes=[ares('qk_r')])
                        P.op('dve', CALL('tensor_tensor', out=d_rp1, in0=t4[:, :, 0, :], in1=sinb, op=ALU.mult), reads=[ares('tq'), R['sinT']], writes=[rr])
                        P.op(ROPE_ENG, CALL('tensor_tensor', out=d_rp2, in0=t4[:, :, 1, :], in1=cosb, op=ALU.mult), reads=[ares('tq'), R['cosT']], writes=[rr2])
                        P.op('dve', CALL('tensor_tensor', out=o4[:, :, 1, :], in0=d_rp1, in1=d_rp2, op=ALU.add), reads=[rr, rr2], writes=[ares('qk_r')])
                        P.op('act', CALL('activation', out=L['kd'][:, c, :], in_=d_qkr[:, c, 128:256], func=AF.Identity, scale=cols[:, C_KDEC + h:C_KDEC + h + 1]),
                             reads=[ares('qk_r'), R['cols']], writes=[ares('d_kd%d' % par)])
                        yield
                    for _ in gen:
                        pass
                    vv = L['vv']
                    yield from tm_proj(('tm', 'rtv', 3072 + h * 256), 3072 + h * 256, lambda c: vv[:, c, :], ares('d_vv%d' % par))
                    for half in range(2):
                        fbz = FB_RTZ + 2 * h + half
                        pb, rpb = fm_proj(fbz)
                        P.op('act', CALL('activation', out=L['sZ'][:, half, :], in_=pb[:, 0:T], func=AF.Silu, bias=cols[:, C_BFM + fbz:C_BFM + fbz + 1]),
                             reads=[rpb, R['cols']], writes=[ares('d_sZ%d' % par)])
                        yield
                    pq, rpq = bank()
                    pqb = pq[:].bitcast(BF16)
                    for c in range(NB):
                        P.op('pe', CALL('transpose', out=pqb[:, c * 128:(c + 1) * 128], in_=d_qkr[:, c, 0:128], identity=identb[:]), reads=[ares('qk_r'), R['identb']], writes=[rpq])
                        P.op('pe', CALL('transpose', out=pqb[:, T + c * 128:T + (c + 1) * 128], in_=d_qkr[:, c, 128:256], identity=identb[:]), reads=[ares('qk_r'), R['identb']], writes=[rpq])
                    P.op('act', CALL('copy', out=L['qT'], in_=pqb[:, 0:T]), reads=[rpq], writes=[ares('d_qT%d' % par)])
                    P.op('act', CALL('copy', out=L['kT'], in_=pqb[:, T:2 * T]), reads=[rpq], writes=[ares('d_kT%d' % par)])
                    P.op('act', CALL('copy', out=d_qf, in_=pqb[:, 0:T]), reads=[rpq], writes=[ares('qf32')])
                    P.op('dve', CALL('tensor_tensor', out=L['qdT'].rearrange("p (c n) -> p c n", n=128), in0=d_qf.rearrange("p (c n) -> p c n", n=128),
                                     in1=qdec[:, h, :].unsqueeze(1).to_broadcast([128, NB, 128]), op=ALU.mult), reads=[ares('qf32'), R['qdec']], writes=[ares('d_qdT%d' % par)])
                    yield

                def d_rec(h):
                    par = h % 2
                    L = d_live[par]
                    qT, kT, kd, qdT, vv, sZ = L['qT'], L['kT'], L['kd'], L['qdT'], L['vv'], L['sZ']
                    rq, rk, rkd, rqd, rvv, rsz = [ares('d_%s%d' % (n_, par)) for n_ in ('qT', 'kT', 'kd', 'qdT', 'vv', 'sZ')]

                    def front(c):
                        cp = c % 2
                        cs = slice(c * 128, (c + 1) * 128)
                        ps_s, rps = bank()
                        P.op('pe', CALL('matmul', ps_s[:, 0:128], lhsT=kT[:, cs], rhs=qT[:, cs], start=True, stop=True), reads=[rk, rq], writes=[rps])
                        P.op('dve', CALL('tensor_tensor', out=tmp['sTm'][cp], in0=ps_s[:, 0:128], in1=intraT[:, h, :], op=ALU.mult), reads=[rps, R['intraT']], writes=[ares('sTm%d' % cp)])
                        P.op('act', CALL('copy', out=Rb[:], in_=Rst[:, h, :]), reads=[r_R[h]], writes=[R['Rb']])

                    front(0)
                    yield
                    pending = None
                    for c in range(NB):
                        cp = c % 2
                        cs = slice(c * 128, (c + 1) * 128)
                        sTm = tmp['sTm'][cp]
                        rsT = ares('sTm%d' % cp)
                        pn, rpn = bank()
                        P.op('pe', CALL('matmul', pn[:, 0:256], lhsT=sTm, rhs=vv[:, c, :], start=True, stop=False), reads=[rsT, rvv], writes=[rpn])
                        P.op('pe', CALL('matmul', pn[:, 0:256], lhsT=qdT[:, cs], rhs=Rb[:], start=False, stop=True), reads=[rqd, R['Rb']], writes=[rpn])
                        pkv, rpkv = bank()
                        P.op('pe', CALL('matmul', pkv[:, 0:256], lhsT=kd[:, c, :], rhs=vv[:, c, :], start=True, stop=True), reads=[rkd, rvv], writes=[rpkv])
                        P.op('dve', CALL('scalar_tensor_tensor', out=Rst[:, h, :], in0=Rst[:, h, :], scalar=cols[:, C_CD + h:C_CD + h + 1], in1=pkv[:, 0:256], op0=ALU.mult, op1=ALU.add),
                             reads=[r_R[h], R['cols'], rpkv], writes=[r_R[h]])
                        norm_a(pn, rpn, cp, tmp)
                        if c + 1 < NB:
                            front(c + 1)
                        yield
                        norm_b(None, cp, tmp)
                        if pending is not None:
                            norm_tail(*pending)
                        pending = (cp, tmp, rto[:, 2 * h:2 * h + 2, :], R['rto'], C_RTG + 2 * h, sZ, rsz, c)
                        yield
                    norm_tail(*pending)
                    yield

                AR.reset(OFF_ET)
                e_pp = [AR.alloc([256], F32) for _ in range(2)]
                e_pnb = [AR.alloc([256], BF16) for _ in range(2)]
                assert AR.off <= OFF_L0, AR.off
                e_live = []
                for par_ in range(2):
                    AR.reset(OFF_L0 + par_ * LIVE_SZ)
                    e_live.append(dict(xqT=AR.alloc([2, T], BF16), xsz=AR.alloc([2, T], BF16), pT=AR.alloc([2, T], BF16)))

                def e_proj(h):
                    par = h % 2
                    if tt == 0:
                        pump(12)
                    L = e_live[par]
                    for half in range(2):
                        fb = FB_XAQ + 2 * h + half
                        pb, rpb = fm_proj(fb)
                        P.op('act', CALL('activation', out=L['xqT'][:, half, :], in_=pb[:, 0:T], func=AF.Identity, bias=cols[:, C_BFM + fb:C_BFM + fb + 1]),
                             reads=[rpb, R['cols']], writes=[ares('xqT%d' % par)])
                        yield
                    for half in range(2):
                        fb = FB_XAZ + 2 * h + half
                        pb, rpb = fm_proj(fb)
                        P.op('act', CALL('activation', out=L['xsz'][:, half, :], in_=pb[:, 0:T], func=AF.Silu, bias=cols[:, C_BFM + fb:C_BFM + fb + 1]),
                             reads=[rpb, R['cols']], writes=[ares('xsz%d' % par)])
                        yield

                def e_rec(h):
                    par = h % 2
                    L = e_live[par]
                    xqT, xsz, pT = L['xqT'], L['xsz'], L['pT']
                    rxq, rxs, rpT = ares('xqT%d' % par), ares('xsz%d' % par), ares('pT%d' % par)
                    for c in range(NB):
                        cp = c % 2
                        cs = slice(c * 128, (c + 1) * 128)
                        pp, pnb = e_pp[cp], e_pnb[cp]
                        rS = R['stat_x%d' % cp]
                        psc, rpsc = bank()
                        for half in range(2):
                            P.op('pe', CALL('matmul', psc[:, 0:256], lhsT=xqT[:, half, cs], rhs=mkT[:, 2 * h + half, :], start=(half == 0), stop=(half == 1)),
                                 reads=[rxq, R['mkT']], writes=[rpsc])
                        sm = stat[:, 48 + 4 * cp:52 + 4 * cp]
                        P.op('dve', CALL('tensor_reduce', out=sm[:, 0:1], in_=psc[:, 0:256], axis=AX.X, op=ALU.max), reads=[rpsc], writes=[rS])
                        P.op('dve', CALL('tensor_scalar', out=sm[:, 1:2], in0=sm[:, 0:1], scalar1=-1.0 / 16, scalar2=None, op0=ALU.mult), reads=[rS], writes=[rS])
                        P.op('act', CALL('activation', out=pp, in_=psc[:, 0:256], func=AF.Exp, scale=1.0 / 16, bias=sm[:, 1:2], accum_out=sm[:, 2:3]),
                             reads=[rpsc, rS], writes=[ares('pp%d' % cp), rS])
                        P.op('dve', CALL('reciprocal', out=sm[:, 3:4], in_=sm[:, 2:3]), reads=[rS], writes=[rS])
                        P.op('dve', CALL('tensor_scalar', out=pnb, in0=pp, scalar1=sm[:, 3:4], scalar2=None, op0=ALU.mult), reads=[ares('pp%d' % cp), rS], writes=[ares('pnb%d' % cp)])
                        yield
                        ptp, rptp = bank()
                        ptpb = ptp[:].bitcast(BF16)
                        for mh in range(2):
                            P.op('pe', CALL('transpose', out=ptpb[:, mh * 128:(mh + 1) * 128], in_=pnb[:, mh * 128:(mh + 1) * 128], identity=identb[:]),
                                 reads=[ares('pnb%d' % cp), R['identb']], writes=[rptp])
                        P.op('act', CALL('copy', out=pT[:, :, cs], in_=ptpb[:, 0:256].rearrange("p (a b) -> p a b", b=128)), reads=[rptp], writes=[rpT])
                        yield
                    for dh in range(2):
                        po, rpo = bank()
                        for mh in range(2):
                            P.op('pe', CALL('matmul', po[:, 0:T], lhsT=mvv[:, mh, h * 256 + dh * 128:h * 256 + (dh + 1) * 128], rhs=pT[:, mh, :], start=(mh == 0), stop=(mh == 1)),
                                 reads=[R['mvv'], rpT], writes=[rpo])
                        P.op('dve', CALL('tensor_tensor', out=xao[:, 2 * h + dh, :], in0=po[:, 0:T], in1=xsz[:, dh, :], op=ALU.mult), reads=[rpo, rxs], writes=[R['xao']])
                        yield

                stages = ([('C', (lambda h=h: c_proj(h)), (lambda h=h: c_rec(h))) for h in range(6)]
                          + [('D', (lambda h=h: d_proj(h)), (lambda h=h: d_rec(h))) for h in range(6)]
                          + [('E', (lambda h=h: e_proj(h)), (lambda h=h: e_rec(h))) for h in range(4)])
                CE = ('pe', 'act', 'dve', 'pool')

                def snapshot():
                    snap = []
                    for f in CE:
                        n = len(P.ops[f])
                        while n > 0 and (P.ops[f][n - 1]['fn'] is None or P.ops[f][n - 1]['dma'] is not None):
                            n -= 1
                        if n > 0:
                            snap.append((f, n))
                    return snap

                prev_st = [last_store[k_] for k_ in (2, 3) if k_ in last_store]
                if prev_st:
                    for e_ in CE:
                        P.wait_events(e_, prev_st)
                for _ in stages[0][1]():
                    pass
                for i_ in range(len(stages)):
                    snap = snapshot()
                    a_ = stages[i_][2]()
                    b_ = None
                    if i_ + 1 < len(stages):
                        if stages[i_ + 1][0] != stages[i_][0] or (i_ >= 1 and stages[i_ + 1][0] != stages[i_ - 1][0]):
                            for e_ in CE:
                                P.wait_events(e_, snap)
                        b_ = stages[i_ + 1][1]()
                    interleave(a_, b_)
                ckpt('E')
                snapF = snapshot()
                P.barrier()
                P.wait_events('sp', snapF)
                AR.reset()
                A.clear()
                mgT = AR.alloc([KC, T], BF16)
                tg = AR.alloc([T], BF16)
                m0 = AR.alloc([T], F32)
                m1 = AR.alloc([T], F32)
                assert AR.off <= OFF_ET, AR.off
                AR.reset(OFF_L0)
                ybuf = [AR.alloc([D], F32), AR.alloc([D], F32)]
                srcs = {'ml': (mlo, R['mlo'], 12), 'rt': (rto, R['rto'], 12), 'xa': (xao, R['xao'], 8)}
                for n in range(16):
                    for gi, which in enumerate(['ml', 'rt', 'xa']):
                        src, rsrc, nk = srcs[which]
                        wap, rw = getw(('br', which, n))
                        pbp, rpbp = mm_fm(wap, rw, lambda kc, src=src: src[:, kc, :], [rsrc], nk)
                        relw('wfm')
                        fb = FB_G + gi * 16 + n
                        wap, rw = getw(('fm', fb))
                        pbg, rpbg = mm_fm(wap, rw, lambda kc: hT[:, kc, :], [R['hT']], KC)
                        relw('wfm')
                        P.op('act', CALL('activation', out=tg, in_=pbg[:, 0:T], func=AF.Tanh, scale=0.5, bias=cols[:, C_HBFM + fb:C_HBFM + fb + 1]),
                             reads=[rpbg, R['cols']], writes=[ares('tg')])
                        dstm = m0 if gi == 0 else m1
                        rdst = ares('m0') if gi == 0 else ares('m1')
                        P.op('dve', CALL('scalar_tensor_tensor', out=dstm, in0=tg, scalar=1.0, in1=pbp[:, 0:T], op0=ALU.add, op1=ALU.mult),
                             reads=[ares('tg'), rpbp], writes=[rdst])
                        if gi == 1:
                            P.op('pool', CALL('tensor_tensor', out=m0, in0=m0, in1=m1, op=ALU.add), reads=[ares('m0'), ares('m1')], writes=[ares('m0')])
                        if gi == 2:
                            P.op('pool', CALL('tensor_tensor', out=mgT[:, n, :], in0=m0, in1=m1, op=ALU.add), reads=[ares('m0'), ares('m1')], writes=[ares('mgT')])
                def outproj(pair):
                    blks = [2 * pair, 2 * pair + 1]
                    xsl = []
                    for j, c in enumerate(blks):
                        if pair == 0:
                            xs, rx = load_rows(x_d[tok0 + c * 128:tok0 + (c + 1) * 128, :])
                            sl = (xt_ctr[0] - 1) % 2
                            xsl.append((c, xs, rx, s_xt[sl], sl))
                        else:
                            xs, rx = ybuf[j], r_y[j]
                            P.op('sp', CALL('dma_start', out=xs, in_=x_d[tok0 + c * 128:tok0 + (c + 1) * 128, :]), writes=[rx], dma_sem=s_y[j])
                            xsl.append((c, xs, rx, s_y[j], 2 + j))
                    for q in range(8):
                        wap, rw = getw(('out', q))
                        for (c, xs, rx, sem_, key_) in xsl:
                            pb, rpb = bank()
                            for kc in range(KC):
                                P.op('pe', CALL('matmul', pb[:, 0:256], lhsT=mgT[:, kc, c * 128:(c + 1) * 128], rhs=wap[:, kc, :], start=(kc == 0), stop=(kc == KC - 1)),
                                     reads=[rw, ares('mgT')], writes=[rpb])
                            P.op('dve', CALL('scalar_tensor_tensor', out=xs[:, q * 256:(q + 1) * 256], in0=pb[:, 0:256], scalar=0.5, in1=xs[:, q * 256:(q + 1) * 256], op0=ALU.mult, op1=ALU.add),
                                 reads=[rpb, rx], writes=[rx])
                        relw('wtm')
                        yield
                    for (c, xs, rx, sem_, key_) in xsl:
                        sc = 4 + 2 * (c % 2)
                        ss = stat[:, sc:sc + 1]
                        rs = stat[:, sc + 1:sc + 2]
                        P.op('act', CALL('activation', out=junk[:], in_=xs, func=AF.Square, accum_out=ss), reads=[rx], writes=[R['junk'], R['stat_a']])
                        P.op('dve', CALL('tensor_scalar', out=rs, in0=ss, scalar1=1.0 / D, scalar2=EPS, op0=ALU.mult, op1=ALU.add), reads=[R['stat_a']], writes=[R['stat_a']])
                        P.op('pool', CALL('tensor_tensor', out=rs, in0=rs, in1=mhalf[:], op=ALU.pow), reads=[R['stat_a'], R['mhalf']], writes=[R['stat_a']])
                        P.op('dve', CALL('scalar_tensor_tensor', out=xs, in0=xs, scalar=rs, in1=fgbc[:], op0=ALU.mult, op1=ALU.mult), reads=[rx, R['stat_a'], R['fgbc']], writes=[rx])
                        P.op('sp', CALL('dma_start', out=out_d[tok0 + c * 128:tok0 + (c + 1) * 128, :], in_=xs), reads=[rx], writes=[], dma_sem=sem_)
                        last_store[key_] = (('d', sem_), 16 * P.dma_cnt[sem_])
                        yield

                for _ in outproj(0):
                    pass
                go_ = outproj(1)
                ga_ = phaseA(tt + 1) if tt + 1 < NT else None
                for ch_ in 'ooooaoaaoaooaoaaoaa':
                    g_ = go_ if ch_ == 'o' else ga_
                    if g_ is not None:
                        try:
                            next(g_)
                        except StopIteration:
                            pass
                interleave(go_, ga_)
                P.barrier()

        try:
            body()
        except _Stop:
            pass
        P.wait_events('sp', [v for v in last_store.values()])
        assert kstop is not None or SM.cur == len(SM.specs), (SM.cur, len(SM.specs))
        P.emit(st)
    return nc


_CACHE = {}


def _consts():
    h = np.arange(6)
    gam = 1.0 - 2.0 ** (-5.0 - h)
    lg = np.log(gam)
    l = np.arange(128)
    maskT = (l[:, None] <= l[None, :]).astype(np.float32)
    intraT = np.zeros((128, 6, 128), np.float64)
    for hh in range(6):
        d = (l[None, :] - l[:, None])
        intraT[:, hh, :] = np.where(d >= 0, np.exp(d * lg[hh]), 0.0) * (128.0 ** -0.5)
    qdec = np.zeros((128, 6, 128), np.float64)
    for hh in range(6):
        qdec[:, hh, :] = np.exp((l + 1.0) * lg[hh])[None, :]
    kdec = np.exp((127.0 - l)[:, None] * lg[None, :]) * (128.0 ** -0.5)
    cd = np.broadcast_to(np.exp(128.0 * lg)[None, :], (128, 6))
    freqs = (10000.0 ** (-np.arange(64, dtype=np.float32) / np.float32(64))).astype(np.float32)
    return dict(maskT=maskT, intraT=intraT.reshape(128, 768).astype(np.float32), qdec=qdec.reshape(128, 768).astype(np.float32),
                kdec=kdec.astype(np.float32), cd=np.ascontiguousarray(cd).astype(np.float32),
                freq=np.ascontiguousarray(np.broadcast_to(freqs[None, :], (128, 64))).astype(np.float32))


def _pack(inputs):
    cst = _consts()
    b_in = np.asarray(inputs["b_in"], np.float32)[0]
    cols = np.zeros((128, NCOLS), np.float32)
    for b, off in enumerate(FM_OFFS):
        cols[:, C_BFM + b] = b_in[off:off + 128]
    cw = np.asarray(inputs["conv_w"], np.float32)[0]
    cbias = np.asarray(inputs["conv_b"], np.float32)[0]
    for cb in range(12):
        for k in range(4):
            cols[:, C_CONVW + cb * 4 + k] = cw[k, cb * 128:(cb + 1) * 128]
        cols[:, C_CONVB + cb] = cbias[cb * 128:(cb + 1) * 128]
    cols[:, C_MLG:C_MLG + 12] = np.asarray(inputs["ml_hnorm_g"], np.float32)[0].reshape(12, 128).T
    cols[:, C_RTG:C_RTG + 12] = np.asarray(inputs["ret_hnorm_g"], np.float32)[0].reshape(12, 128).T
    cols[:, C_GCOL:C_GCOL + 16] = np.asarray(inputs["ln_g"], np.float32)[0].reshape(16, 128).T
    cols[:, C_MGCOL:C_MGCOL + 16] = np.asarray(inputs["mem_ln_g"], np.float32)[0].reshape(16, 128).T
    cols[0:6, C_BI] = b_in[O_MLI:O_MLI + 6]
    cols[0:6, C_BF] = b_in[O_MLF:O_MLF + 6]
    cols[:, C_KDEC:C_KDEC + 6] = cst["kdec"]
    cols[:, C_CD:C_CD + 6] = cst["cd"]
    rows = np.zeros((1, NTM), np.float32)
    for (dc, sc, n) in TM_SEGS:
        rows[0, dc:dc + n] = b_in[sc:sc + n]
    shared = dict(
        w_in=np.ascontiguousarray(np.asarray(inputs["w_in"], np.float32)[0]),
        w_kv=np.ascontiguousarray(np.asarray(inputs["w_mem_kv"], np.float32)[0]),
        w_ml=np.ascontiguousarray(np.asarray(inputs["w_br_ml"], np.float32)[0]),
        w_rt=np.ascontiguousarray(np.asarray(inputs["w_br_ret"], np.float32)[0]),
        w_xa=np.ascontiguousarray(np.asarray(inputs["w_br_xa"], np.float32)[0]),
        w_out=np.ascontiguousarray(np.asarray(inputs["w_out"], np.float32)[0]),
        cols=cols, rows=rows,
        fg=np.ascontiguousarray(np.asarray(inputs["final_g"], np.float32).reshape(1, D)),
        identb=np.eye(128).astype(ml_dtypes.bfloat16), identf=np.eye(128, dtype=np.float32),
        maskT=cst["maskT"], intraT=cst["intraT"], qdec=cst["qdec"], freq=cst["freq"],
    )
    return shared


def make_in_maps(inputs):
    x = np.asarray(inputs["x"], np.float32)
    mem = np.asarray(inputs["mem"], np.float32)
    pos = np.asarray(inputs["positions"], np.int32)
    B, S, _ = x.shape
    shared = _pack(inputs)
    in_maps = []
    for b in range(B):
        m = dict(shared)
        m["x"] = np.ascontiguousarray(x[b])
        m["mem"] = np.ascontiguousarray(mem[b])
        m["pos"] = np.ascontiguousarray(pos[b].reshape(S // 128, 128).T)
        in_maps.append(m)
    return in_maps, B, S


def kernel(**inputs):
    in_maps, B, S = make_in_maps(inputs)
    if S not in _CACHE:
        _CACHE[S] = build_program(S)
    nc = _CACHE[S]
    res = run_bass_kernel_spmd(nc, in_maps, core_ids=list(range(B)))
    out = np.stack([np.asarray(r["out"], np.float32) for r in res.results], axis=0)
    return out
```

```python
import contextlib
import math
import numpy as np
import ml_dtypes
import concourse.bass as bass
import concourse.mybir as mybir
from concourse.bass_utils import run_bass_kernel_spmd

F32 = mybir.dt.float32
BF16 = mybir.dt.bfloat16
I32 = mybir.dt.int32
AF = mybir.ActivationFunctionType
ALU = mybir.AluOpType
AX = mybir.AxisListType

ENGS = ['pe', 'act', 'dve', 'pool', 'sp']
EPOCH = 30000
BIG = 10 ** 9

D = 2048
NIN = 18956
EPS = 1e-6
T = 512
NB = 4
KC = 16
O_MLQ, O_MLK, O_MLV, O_MLO, O_MLZ, O_MLI, O_MLF = 0, 768, 1536, 3072, 4608, 6144, 6150
O_RTQ, O_RTK, O_RTV, O_RTZ, O_XAQ, O_XAZ, O_G = 6156, 6924, 7692, 9228, 10764, 11788, 12812
FB_MLQ, FB_MLK, FB_MLO, FB_MLZ, FB_RTZ, FB_XAQ, FB_XAZ, FB_G = 0, 6, 12, 24, 36, 48, 56, 64
NFB = 112
FM_OFFS = ([O_MLQ + i * 128 for i in range(6)] + [O_MLK + i * 128 for i in range(6)]
           + [O_MLO + i * 128 for i in range(12)] + [O_MLZ + i * 128 for i in range(12)]
           + [O_RTZ + i * 128 for i in range(12)] + [O_XAQ + i * 128 for i in range(8)]
           + [O_XAZ + i * 128 for i in range(8)] + [O_G + i * 128 for i in range(48)])
TM_SEGS = ([(0, O_MLV, 1536)]
           + [(1536 + h * 256, O_RTQ + h * 128, 128) for h in range(6)]
           + [(1536 + h * 256 + 128, O_RTK + h * 128, 128) for h in range(6)]
           + [(3072, O_RTV, 1536)])
NTM = 4608
C_BFM = 0
C_HBFM = 112
C_CONVW = 224
C_CONVB = 272
C_MLG = 284
C_RTG = 296
C_GCOL = 308
C_MGCOL = 324
C_BI = 340
C_BF = 341
C_KDEC = 342
C_CD = 348
NCOLS = 354

PI = math.pi
TWO_PI_HI = 6.28125
TWO_PI_LO = 2.0 * math.pi - 6.28125
PI_CLAMP = 3.1415925
ROPE_ENG = 'dve'
OFF_PT = 13312
OFF_NT = OFF_PT + 12800
OFF_ET = OFF_NT + 3584
OFF_L0 = OFF_ET + 3072
LIVE_SZ = 8192


def CALL(name, *args, **kwargs):
    def f(e):
        return getattr(e, name)(*args, **kwargs)
    return f


class Res:
    __slots__ = ('name', 'w', 'r')

    def __init__(self, name=''):
        self.name = name
        self.w = None
        self.r = {}


class Prog:
    def __init__(self, nc):
        self.nc = nc
        self.ops = {e: [] for e in ENGS}
        self.seen = {e: {} for e in ENGS}
        self.dma_cnt = []
        self.dma_total_only = []

    def new_dma_sem(self, total_only=False):
        self.dma_cnt.append(0)
        self.dma_total_only.append(total_only)
        return len(self.dma_cnt) - 1

    def _dep(self, e, ev, waits):
        key, val = ev
        if key == e and e == 'pe':
            return
        if self.seen[e].get(key, 0) >= val:
            return
        if waits.get(key, 0) < val:
            waits[key] = val

    def _commit(self, e, waits):
        for k, v in waits.items():
            self.seen[e][k] = v
            if not isinstance(k, tuple):
                self.ops[k][v - 1]['inc'] = True

    def op(self, e, fn, reads=(), writes=(), dma_sem=None):
        waits = {}
        for r in reads:
            if r.w is not None:
                self._dep(e, r.w, waits)
        for w in writes:
            if w.w is not None:
                self._dep(e, w.w, waits)
            for k, v in w.r.items():
                self._dep(e, (k, v), waits)
        self._commit(e, waits)
        idx = len(self.ops[e])
        if dma_sem is None:
            ev = (e, idx + 1)
        else:
            self.dma_cnt[dma_sem] += 1
            ev = (('d', dma_sem), BIG if self.dma_total_only[dma_sem] else 16 * self.dma_cnt[dma_sem])
        self.ops[e].append(dict(waits=waits, fn=fn, dma=dma_sem, inc=False))
        for r in reads:
            if r.r.get(ev[0], 0) < ev[1]:
                r.r[ev[0]] = ev[1]
        for w in writes:
            w.w = ev
            w.r = {}
        return ev

    def wait_events(self, e, events):
        waits = {}
        for ev in events:
            self._dep(e, ev, waits)
        self._commit(e, waits)
        self.ops[e].append(dict(waits=waits, fn=None, dma=None, inc=False))

    def barrier(self, engines=('pe', 'act', 'dve', 'pool')):
        evs = []
        for f in engines:
            n = len(self.ops[f])
            while n > 0 and (self.ops[f][n - 1]['fn'] is None or self.ops[f][n - 1]['dma'] is not None):
                n -= 1
            if n > 0:
                evs.append((f, n))
        for e in engines:
            self.wait_events(e, [ev for ev in evs if ev[0] != e])

    def emit(self, stack):
        nc = self.nc
        ords = {}
        nsem = {}
        for e in ENGS:
            o = 0
            lst = []
            for op in self.ops[e]:
                if op['inc']:
                    o += 1
                lst.append(o)
            ords[e] = lst
            nsem[e] = (o + EPOCH - 1) // EPOCH
        sems = {e: [stack.enter_context(nc.semaphore(f"s_{e}_{i}")) for i in range(nsem[e])] for e in ENGS}
        dsems = [stack.enter_context(nc.semaphore(f"d_{i}")) for i in range(len(self.dma_cnt))]
        block = stack.enter_context(nc.Block())

        def run(e, eng):
            for i, op in enumerate(self.ops[e]):
                for k, v in op['waits'].items():
                    if isinstance(k, tuple):
                        eng.wait_ge(dsems[k[1]], 16 * self.dma_cnt[k[1]] if v == BIG else v)
                    else:
                        o = ords[k][v - 1]
                        ep = (o - 1) // EPOCH
                        eng.wait_ge(sems[k][ep], o - ep * EPOCH)
                if op['fn'] is None:
                    continue
                inst = op['fn'](eng)
                if op['dma'] is not None:
                    inst.then_inc(dsems[op['dma']], 16)
                elif op['inc']:
                    o = ords[e][i]
                    ep = (o - 1) // EPOCH
                    inst.then_inc(sems[e][ep], 1)

        @block.tensor
        def _(eng):
            run('pe', eng)

        @block.scalar
        def _(eng):
            run('act', eng)

        @block.vector
        def _(eng):
            run('dve', eng)

        @block.gpsimd
        def _(eng):
            run('pool', eng)

        @block.sync
        def _(eng):
            run('sp', eng)


class Arena:
    def __init__(self, ap, nbytes):
        self.ap = ap
        self.nbytes = nbytes
        self.off = 0

    def reset(self, off=0):
        self.off = off

    def alloc(self, free_shape, dt):
        esz = 4 if dt in (F32, I32) else 2
        n = 1
        for s in free_shape:
            n *= s
        nb = n * esz
        self.off = (self.off + 31) // 32 * 32
        assert self.off + nb <= self.nbytes, f"arena overflow {self.off}+{nb}>{self.nbytes}"
        v = self.ap[:, self.off // 2:(self.off + nb) // 2]
        if esz == 4:
            v = v.bitcast(dt)
        self.off += nb
        if len(free_shape) == 2:
            v = v.rearrange("p (a b) -> p a b", b=free_shape[1])
        elif len(free_shape) == 3:
            v = v.rearrange("p (a b c) -> p a b c", b=free_shape[1], c=free_shape[2])
        return v


class Ring:
    def __init__(self, slots):
        self.slots = slots
        self.n = len(slots)
        self.next = 0


class Stream:
    def __init__(self, P, rings, sems):
        self.P = P
        self.rings = rings
        self.sems = sems
        self.specs = []
        self.cur = 0
        self.issued = 0
        self.inflight = {k: 0 for k in rings}
        self.slot_of = {}

    def add(self, ring, key, fns):
        self.specs.append((ring, key, fns))

    def _issue(self):
        HORIZON = 48
        if not hasattr(self, 'scan'):
            self.scan = 0
        while self.scan < len(self.specs) and self.scan in self.slot_of:
            self.scan += 1
        blocked = set()
        i = self.scan
        end = min(len(self.specs), self.cur + HORIZON)
        n_t0 = getattr(self, 'n_tile0', 0)
        while i < end and len(blocked) < len(self.rings):
            if i in self.slot_of:
                i += 1
                continue
            ring, key, fns = self.specs[i]
            R = self.rings[ring]
            if ring in blocked:
                i += 1
                continue
            if self.inflight[ring] >= R.n or (i < n_t0 and key not in self.cast_sem):
                blocked.add(ring)
                i += 1
                continue
            s = R.next
            R.next = (R.next + 1) % R.n
            ap, res = R.slots[s]
            if i < n_t0:
                self.P.wait_events('sp', [(('d', self.cast_sem[key]), BIG)])
            for j, f in enumerate(fns):
                fn = f(ap)
                if j == 0:
                    self.P.op('sp', fn, writes=[res], dma_sem=self.sems[ring][s])
                else:
                    ev = self.P.op('sp', fn, writes=[], dma_sem=self.sems[ring][s])
                    res.w = ev
            self.slot_of[i] = s
            self.inflight[ring] += 1
            i += 1

    def get(self, key):
        assert self.cur < len(self.specs), f"stream exhausted at {key}"
        ring, k, _ = self.specs[self.cur]
        assert k == key, f"stream order mismatch: expected {k}, got {key}"
        self._issue()
        assert self.cur in self.slot_of, f"item {key} could not be issued (cast not pumped or ring full)"
        s = self.slot_of[self.cur]
        self.cur += 1
        return self.rings[ring].slots[s]

    def release(self, ring):
        self.inflight[ring] -= 1
        self._issue()


class _Stop(Exception):
    pass


def build_program(S, kstop=None):
    def ckpt(name):
        if kstop == name:
            raise _Stop()
    NT = S // T
    NBLK = S // 128
    nc = bass.Bass("TRN2", target_bir_lowering=False)

    def din(name, shape, dt=F32):
        return nc.dram_tensor(name, shape, dt, kind="ExternalInput").ap()
    x_d = din("x", [S, D])
    mem_d = din("mem", [256, D])
    pos_d = din("pos", [128, NBLK], I32)
    w_in_d = din("w_in", [D, NIN])
    w_kv_d = din("w_kv", [D, 2048])
    w_ml_d = din("w_ml", [1536, D])
    w_rt_d = din("w_rt", [1536, D])
    w_xa_d = din("w_xa", [1024, D])
    w_out_d = din("w_out", [D, D])
    cols_d = din("cols", [128, NCOLS])
    rows_d = din("rows", [1, NTM])
    fg_d = din("fg", [1, D])
    identb_d = din("identb", [128, 128], BF16)
    identf_d = din("identf", [128, 128])
    maskT_d = din("maskT", [128, 128])
    intraT_d = din("intraT", [128, 768])
    qdec_d = din("qdec", [128, 768])
    freq_d = din("freq", [128, 64])
    out_d = nc.dram_tensor("out", [S, D], F32, kind="ExternalOutput").ap()
    s_fm = nc.dram_tensor("s_fm", [NFB, 128, KC, 128], BF16).ap()
    s_tm = nc.dram_tensor("s_tm", [128, KC, NTM], BF16).ap()
    s_ml = nc.dram_tensor("s_ml", [16, 128, 12, 128], BF16).ap()
    s_rt = nc.dram_tensor("s_rt", [16, 128, 12, 128], BF16).ap()
    s_xa = nc.dram_tensor("s_xa", [16, 128, 8, 128], BF16).ap()
    s_out = nc.dram_tensor("s_out", [128, KC, D], BF16).ap()

    P = Prog(nc)
    st = contextlib.ExitStack()
    with st:
        def sb(name, shape, dt):
            return st.enter_context(nc.sbuf_tensor(name, shape, dt))
        hT = sb("hT", [128, KC, T], BF16)
        mlo = sb("mlo", [128, 12, T], BF16)
        rto = sb("rto", [128, 12, T], BF16)
        xao = sb("xao", [128, 8, T], BF16)
        NWFM, NWTM = 5, 3
        wfm_t = sb("wfm", [128, NWFM, KC, 128], BF16)
        wtm_t = sb("wtm", [128, NWTM, KC, 256], BF16)
        brow_t = sb("brow", [128, 2, 256], F32)
        xt_t = sb("xt", [128, 2, D], F32)
        junk = sb("junk", [128, D], BF16)
        junk2 = sb("junk2", [128, 512], BF16)
        fgbc = sb("fgbc", [128, D], F32)
        cols = sb("cols_sb", [128, NCOLS], F32)
        identb = sb("identb_sb", [128, 128], BF16)
        identf = sb("identf_sb", [128, 128], F32)
        maskT = sb("maskT_sb", [128, 128], F32)
        intraT = sb("intraT_sb", [128, 6, 128], F32)
        qdec = sb("qdec_sb", [128, 6, 128], F32)
        freq = sb("freq_sb", [128, 64], F32)
        posi = sb("posi", [128, NBLK], I32)
        posf = sb("posf", [128, NBLK], F32)
        wif = sb("wif", [128, KC, 12], BF16)
        mkT = sb("mkT", [128, 8, 256], BF16)
        mvv = sb("mvv", [128, 2, 1024], BF16)
        Cst = sb("Cst", [128, 6, 257], F32)
        Cb = sb("Cb", [128, 257], BF16)
        Rst = sb("Rst", [128, 6, 256], F32)
        Rb = sb("Rb", [128, 256], BF16)
        carry = sb("carry", [128, 12, 3], F32)
        mstate = sb("mstate", [128, 1], F32)
        mhalf = sb("mhalf", [128, 1], F32)
        ones6 = sb("ones6", [128, 128], F32)
        cosT = sb("cosT", [128, NB, 64], F32)
        sinT = sb("sinT", [128, NB, 64], F32)
        stat = sb("stat", [128, 64], F32)
        ARENA_BYTES = 50 * 1024
        arena_t = sb("arena", [128, ARENA_BYTES // 2], BF16)
        AR = Arena(arena_t, ARENA_BYTES)
        psum = [st.enter_context(nc.psum_tensor(f"ps{i}", [128, 512], F32)) for i in range(8)]
        r_ps = [Res(f"ps{i}") for i in range(8)]
        bank_ctr = [0]

        def bank():
            b = bank_ctr[0] % 8
            bank_ctr[0] += 1
            return psum[b], r_ps[b]

        R = {k: Res(k) for k in ['hT', 'mlo', 'rto', 'xao', 'junk', 'fgbc', 'cols', 'identb', 'identf', 'maskT', 'intraT',
                                 'qdec', 'freq', 'posi', 'posf', 'wif', 'mkT', 'mvv', 'Cb', 'Rb', 'carry', 'mstate',
                                 'mhalf', 'ones6', 'cosT', 'sinT', 'stat_a', 'stat_h0', 'stat_h1', 'stat_x0', 'stat_x1', 'junk0', 'junk1', 'scratch']}
        r_C = [Res(f"C{h}") for h in range(6)]
        r_R = [Res(f"R{h}") for h in range(6)]
        r_xt = [Res("xt0"), Res("xt1")]
        r_brow = [Res("br0"), Res("br1")]
        r_hTb = [Res(f"hTb{b}") for b in range(NB)]
        A = {}

        def ares(name):
            if name not in A:
                A[name] = Res(name)
            return A[name]

        s_const = P.new_dma_sem(total_only=True)
        s_xt = [P.new_dma_sem(), P.new_dma_sem()]
        s_brow = [P.new_dma_sem(), P.new_dma_sem()]
        s_pre = P.new_dma_sem()
        s_y = [P.new_dma_sem(), P.new_dma_sem()]
        r_y = [Res('y0'), Res('y1')]
        rings = {
            'wfm': Ring([(wfm_t[:, i], Res(f"wfm{i}")) for i in range(NWFM)]),
            'wtm': Ring([(wtm_t[:, i], Res(f"wtm{i}")) for i in range(NWTM)]),
        }
        ring_sems = {k: [P.new_dma_sem() for _ in range(r.n)] for k, r in rings.items()}

        def cload(dst, src, res):
            P.op('sp', CALL('dma_start', out=dst, in_=src), writes=[res], dma_sem=s_const)
        cload(cols[:], cols_d, R['cols'])
        cload(identb[:], identb_d, R['identb'])
        cload(identf[:], identf_d, R['identf'])
        cload(maskT[:], maskT_d, R['maskT'])
        cload(intraT[:], intraT_d.rearrange("p (h l) -> p h l", l=128), R['intraT'])
        cload(qdec[:], qdec_d.rearrange("p (h l) -> p h l", l=128), R['qdec'])
        cload(freq[:], freq_d, R['freq'])
        cload(posi[:], pos_d, R['posi'])
        cload(fgbc[:], fg_d[0:1, :].to_broadcast([128, D]), R['fgbc'])
        P.op('pool', CALL('dma_start', out=wif[:], in_=w_in_d[:, O_MLI:O_MLI + 12].rearrange("(kc p) n -> p kc n", p=128)),
             writes=[R['wif']], dma_sem=s_pre)
        P.op('dve', CALL('memset', mhalf[:], -0.5), writes=[R['mhalf']])
        P.op('dve', CALL('memset', ones6[:], 1.0), writes=[R['ones6']])
        P.op('dve', CALL('memset', mstate[:], 0.0), writes=[R['mstate']])
        P.op('dve', CALL('memset', carry[:], 0.0), writes=[R['carry']])
        P.op('dve', CALL('memset', Cst[:], 0.0), writes=r_C)
        P.op('dve', CALL('memset', Rst[:], 0.0), writes=r_R)
        P.op('dve', CALL('tensor_scalar', out=cols[:, C_HBFM:C_HBFM + NFB], in0=cols[:, C_BFM:C_BFM + NFB], scalar1=0.5, scalar2=None, op0=ALU.mult),
             reads=[R['cols']], writes=[R['cols']])
        P.op('dve', CALL('tensor_scalar', out=cols[:, C_MLG:C_MLG + 12], in0=cols[:, C_MLG:C_MLG + 12], scalar1=0.5, scalar2=None, op0=ALU.mult),
             reads=[R['cols']], writes=[R['cols']])
        P.op('dve', CALL('tensor_scalar', out=cols[:, C_BF:C_BF + 1], in0=cols[:, C_BF:C_BF + 1], scalar1=-1.0, scalar2=None, op0=ALU.mult),
             reads=[R['cols']], writes=[R['cols']])
        P.op('dve', CALL('tensor_copy', out=posf[:], in_=posi[:]), reads=[R['posi']], writes=[R['posf']])

        xt_ctr = [0]

        def load_rows(src_ap):
            s = xt_ctr[0] % 2
            xt_ctr[0] += 1
            ap = xt_t[:, s]
            P.op('sp', CALL('dma_start', out=ap, in_=src_ap), writes=[r_xt[s]], dma_sem=s_xt[s])
            return ap, r_xt[s]

        def rms_scale(xs, rx, scol):
            ss = stat[:, scol:scol + 1]
            rs = stat[:, scol + 1:scol + 2]
            P.op('act', CALL('activation', out=junk[:], in_=xs, func=AF.Square, accum_out=ss), reads=[rx], writes=[R['junk'], R['stat_a']])
            P.op('dve', CALL('tensor_scalar', out=rs, in0=ss, scalar1=1.0 / D, scalar2=EPS, op0=ALU.mult, op1=ALU.add), reads=[R['stat_a']], writes=[R['stat_a']])
            P.op('pool', CALL('tensor_tensor', out=rs, in0=rs, in1=mhalf[:], op=ALU.pow), reads=[R['stat_a'], R['mhalf']], writes=[R['stat_a']])
            P.op('dve', CALL('tensor_scalar', out=xs, in0=xs, scalar1=rs, scalar2=None, op0=ALU.mult), reads=[rx, R['stat_a']], writes=[rx])

        def transpose_rows(xs, rx, dst_fn, gbase, rdst):
            for q in range(4):
                pb, rpb = bank()
                for j in range(4):
                    kc = q * 4 + j
                    P.op('pe', CALL('transpose', out=pb[:, j * 128:(j + 1) * 128], in_=xs[:, kc * 128:(kc + 1) * 128], identity=identf[:]),
                         reads=[rx, R['identf']], writes=[rpb])
                eng = 'dve' if q % 2 == 0 else 'pool'
                eng = 'dve'
                P.op(eng, CALL('tensor_tensor', out=dst_fn(q * 4), in0=pb[:].rearrange("p (a b) -> p a b", b=128),
                                                                 in1=cols[:, gbase + q * 4:gbase + q * 4 + 4].unsqueeze(2).to_broadcast([128, 4, 128]), op=ALU.mult),
                     reads=[rpb, R['cols']], writes=[rdst])

        def mm_fm(wap, rw, rhs_fn, r_rhs, nk, n=T, m=128):
            pb, rpb = bank()
            for kc in range(nk):
                P.op('pe', CALL('matmul', pb[0:m, 0:n], lhsT=wap[:, kc, 0:m], rhs=rhs_fn(kc), start=(kc == 0), stop=(kc == nk - 1)),
                     reads=[rw] + r_rhs, writes=[rpb])
            return pb, rpb

        SM = Stream(P, rings, ring_sems)

        def fm_spec(b):
            SM.add('wfm', ('fm', b), [lambda ap, b=b: (CALL('dma_start', out=ap, in_=s_fm[b]))])

        def tm_spec(c0, tag):
            SM.add('wtm', ('tm', tag, c0), [lambda ap, c0=c0: (CALL('dma_start', out=ap, in_=s_tm[:, :, c0:c0 + 256]))])

        def br_spec(which, n):
            src = {'ml': s_ml, 'rt': s_rt, 'xa': s_xa}[which]
            nk = 8 if which == 'xa' else 12
            SM.add('wfm', ('br', which, n), [lambda ap, n=n, nk=nk, src=src: (CALL('dma_start', out=ap[:, 0:nk, :], in_=src[n]))])

        def out_spec(q):
            SM.add('wtm', ('out', q), [lambda ap, q=q: (CALL('dma_start', out=ap, in_=s_out[:, :, q * 256:(q + 1) * 256]))])

        for tt in range(NT):
            for h in range(6):
                fm_spec(FB_MLQ + h); fm_spec(FB_MLK + h)
                tm_spec(h * 256, 'mlv')
                for half in range(2):
                    fm_spec(FB_MLO + 2 * h + half); fm_spec(FB_MLZ + 2 * h + half)
            for h in range(6):
                tm_spec(1536 + h * 256, 'rtqk')
                tm_spec(3072 + h * 256, 'rtv')
                for half in range(2):
                    fm_spec(FB_RTZ + 2 * h + half)
            for h in range(4):
                for half in range(2):
                    fm_spec(FB_XAQ + 2 * h + half)
                for half in range(2):
                    fm_spec(FB_XAZ + 2 * h + half)
            for n in range(16):
                for gi, which in enumerate(['ml', 'rt', 'xa']):
                    br_spec(which, n)
                    fm_spec(FB_G + gi * 16 + n)
            for pair in range(NB // 2):
                for q in range(8):
                    out_spec(q)

        def kc_view(ap):
            return ap.rearrange("(kc p) n -> p kc n", p=128)

        def cast_pairs(key):
            if key[0] == 'fm':
                b_ = key[1]
                return [(s_fm[b_], kc_view(w_in_d[:, FM_OFFS[b_]:FM_OFFS[b_] + 128]))]
            if key[0] == 'tm':
                tag, c0 = key[1], key[2]
                if tag == 'mlv':
                    return [(s_tm[:, :, c0:c0 + 256], kc_view(w_in_d[:, O_MLV + c0:O_MLV + c0 + 256]))]
                if tag == 'rtqk':
                    h_ = (c0 - 1536) // 256
                    return [(s_tm[:, :, c0:c0 + 128], kc_view(w_in_d[:, O_RTQ + h_ * 128:O_RTQ + (h_ + 1) * 128])),
                            (s_tm[:, :, c0 + 128:c0 + 256], kc_view(w_in_d[:, O_RTK + h_ * 128:O_RTK + (h_ + 1) * 128]))]
                off = c0 - 3072
                return [(s_tm[:, :, c0:c0 + 256], kc_view(w_in_d[:, O_RTV + off:O_RTV + off + 256]))]
            if key[0] == 'br':
                which, n_ = key[1], key[2]
                dsts = {'ml': s_ml, 'rt': s_rt, 'xa': s_xa}[which]
                srcs_ = {'ml': w_ml_d, 'rt': w_rt_d, 'xa': w_xa_d}[which]
                return [(dsts[n_], kc_view(srcs_[:, n_ * 128:(n_ + 1) * 128]))]
            q_ = key[1]
            return [(s_out[:, :, q_ * 256:(q_ + 1) * 256], kc_view(w_out_d[:, q_ * 256:(q_ + 1) * 256]))]

        n_tile0 = len(SM.specs) // NT
        cast_pending = []
        _seen_keys = set()
        for (_, key_, _) in SM.specs[:n_tile0]:
            if key_ not in _seen_keys:
                _seen_keys.add(key_)
                cast_pending.append(key_)
        cast_sem = {}

        def pump(n_items):
            done_ = 0
            while done_ < n_items and cast_pending:
                sem_ = P.new_dma_sem(total_only=True)
                for _ in range(min(3, n_items - done_, len(cast_pending))):
                    key_ = cast_pending.pop(0)
                    for (dst_, src_) in cast_pairs(key_):
                        P.op('pool', CALL('dma_start', out=dst_, in_=src_), writes=[], dma_sem=sem_)
                    cast_sem[key_] = sem_
                    done_ += 1
        SM.n_tile0 = n_tile0
        SM.cast_sem = cast_sem
        pump(54)

        def getw(key):
            ap, res = SM.get(key)
            return ap, res

        def relw(ring):
            SM.release(ring)

        brow_ctr = [0]

        def load_brow(c0):
            s = brow_ctr[0] % 2
            brow_ctr[0] += 1
            ap = brow_t[:, s]
            P.op('sp', CALL('dma_start', out=ap, in_=rows_d[0:1, c0:c0 + 256].to_broadcast([128, 256])), writes=[r_brow[s]], dma_sem=s_brow[s])
            return ap, r_brow[s]

        last_store = {}

        def body():
            ckpt('const')
            AR.reset()
            A.clear()
            mnT = AR.alloc([KC, 256], BF16)
            wkv = AR.alloc([2, KC, 256], BF16)
            r_wkv = [Res("wkv0"), Res("wkv1")]
            s_wkv = [P.new_dma_sem(), P.new_dma_sem()]
            for mb in range(2):
                xs, rx = load_rows(mem_d[mb * 128:(mb + 1) * 128, :])
                rms_scale(xs, rx, 0)
                transpose_rows(xs, rx, lambda kc0, mb=mb: mnT[:, kc0:kc0 + 4, mb * 128:(mb + 1) * 128], C_MGCOL, ares('mnT'))
            for piece in range(8):
                s = piece % 2
                P.op('pool', CALL('dma_start', out=wkv[:, s], in_=w_kv_d[:, piece * 256:(piece + 1) * 256].rearrange("(kc p) n -> p kc n", p=128)),
                     writes=[r_wkv[s]], dma_sem=s_wkv[s])
                if piece < 4:
                    for j in range(2):
                        pb, rpb = bank()
                        for kc in range(KC):
                            P.op('pe', CALL('matmul', pb[:, 0:256], lhsT=wkv[:, s, kc, j * 128:(j + 1) * 128], rhs=mnT[:, kc, :], start=(kc == 0), stop=(kc == KC - 1)),
                                 reads=[r_wkv[s], ares('mnT')], writes=[rpb])
                        P.op('act', CALL('copy', out=mkT[:, piece * 2 + j, :], in_=pb[:, 0:256]), reads=[rpb], writes=[R['mkT']])
                else:
                    for mb in range(2):
                        pb, rpb = bank()
                        for kc in range(KC):
                            P.op('pe', CALL('matmul', pb[:, 0:256], lhsT=mnT[:, kc, mb * 128:(mb + 1) * 128], rhs=wkv[:, s, kc, :], start=(kc == 0), stop=(kc == KC - 1)),
                                 reads=[r_wkv[s], ares('mnT')], writes=[rpb])
                        P.op('act', CALL('copy', out=mvv[:, mb, (piece - 4) * 256:(piece - 3) * 256], in_=pb[:, 0:256]), reads=[rpb], writes=[R['mvv']])
            P.barrier()

            LNSC = math.log(128.0 ** -0.5)

            ckpt('pre')
            for tt in range(NT):
                tok0 = tt * T
                def phaseA(t_):
                    RA = dict(ang=Res('a_ang'), kf=Res('a_kf'), cang=Res('a_cang'))
                    tk0 = t_ * T
                    for blk in range(NB):
                        xs, rx = load_rows(x_d[tk0 + blk * 128:tk0 + (blk + 1) * 128, :])
                        rms_scale(xs, rx, 2 * (blk % 2))
                        yield
                        transpose_rows(xs, rx, lambda kc0, blk=blk: hT[:, kc0:kc0 + 4, blk * 128:(blk + 1) * 128], C_GCOL, R['hT'])
                        yield
                    AR.reset(OFF_ET)
                    ang = AR.alloc([NB, 64], F32)
                    kf = AR.alloc([NB, 64], F32)
                    cang = AR.alloc([NB, 64], F32)
                    ki = cang.bitcast(I32)
                    P.op('dve', CALL('tensor_tensor', out=ang, in0=posf[:, t_ * NB:(t_ + 1) * NB].unsqueeze(2).to_broadcast([128, NB, 64]),
                                                          in1=freq[:].unsqueeze(1).to_broadcast([128, NB, 64]), op=ALU.mult),
                         reads=[R['posf'], R['freq']], writes=[RA['ang']])
                    P.op('dve', CALL('tensor_scalar', out=kf, in0=ang, scalar1=1.0 / (2 * PI), scalar2=None, op0=ALU.mult), reads=[RA['ang']], writes=[RA['kf']])
                    P.op('dve', CALL('tensor_copy', out=ki, in_=kf), reads=[RA['kf']], writes=[RA['cang']])
                    P.op('dve', CALL('tensor_copy', out=kf, in_=ki), reads=[RA['cang']], writes=[RA['kf']])
                    P.op('dve', CALL('scalar_tensor_tensor', out=ang, in0=kf, scalar=-TWO_PI_HI, in1=ang, op0=ALU.mult, op1=ALU.add), reads=[RA['kf'], RA['ang']], writes=[RA['ang']])
                    P.op('dve', CALL('scalar_tensor_tensor', out=ang, in0=kf, scalar=-TWO_PI_LO, in1=ang, op0=ALU.mult, op1=ALU.add), reads=[RA['kf'], RA['ang']], writes=[RA['ang']])
                    P.op('dve', CALL('tensor_scalar', out=cang, in0=ang, scalar1=PI / 2, scalar2=None, op0=ALU.add), reads=[RA['ang']], writes=[RA['cang']])
                    P.op('dve', CALL('tensor_scalar', out=kf, in0=cang, scalar1=PI, scalar2=-2 * PI, op0=ALU.is_gt, op1=ALU.mult), reads=[RA['cang']], writes=[RA['kf']])
                    P.op('dve', CALL('tensor_tensor', out=cang, in0=cang, in1=kf, op=ALU.add), reads=[RA['cang'], RA['kf']], writes=[RA['cang']])
                    P.op('dve', CALL('tensor_scalar', out=ang, in0=ang, scalar1=PI_CLAMP, scalar2=-PI_CLAMP, op0=ALU.min, op1=ALU.max), reads=[RA['ang']], writes=[RA['ang']])
                    P.op('dve', CALL('tensor_scalar', out=cang, in0=cang, scalar1=PI_CLAMP, scalar2=-PI_CLAMP, op0=ALU.min, op1=ALU.max), reads=[RA['cang']], writes=[RA['cang']])
                    P.op('act', CALL('activation', out=sinT[:], in_=ang, func=AF.Sin), reads=[RA['ang']], writes=[R['sinT']])
                    P.op('act', CALL('activation', out=cosT[:], in_=cang, func=AF.Sin), reads=[RA['cang']], writes=[R['cosT']])

                    yield

                if tt == 0:
                    for _ in phaseA(0):
                        pass
                A.clear()
                AR.reset(0)
                ckpt('rope')
                g_i = AR.alloc([T], F32)
                g_e = AR.alloc([T], F32)
                g_nb = AR.alloc([T], F32)
                g_a = AR.alloc([T], F32)
                g_w = AR.alloc([T], F32)
                g_fl = AR.alloc([T], F32)
                g_sm = AR.alloc([32], F32)
                Dg = AR.alloc([6, NB], F32)
                gateT = AR.alloc([NB, 16], F32)
                decbc = AR.alloc([6, NB], F32)
                rG = ares('gates')
                pbi, rpbi = mm_fm(wif[:, :, 0:6], R['wif'], lambda kc: hT[:, kc, :], [R['hT']], KC, m=6)
                pbf, rpbf = mm_fm(wif[:, :, 6:12], R['wif'], lambda kc: hT[:, kc, :], [R['hT']], KC, m=6)
                P.op('act', CALL('activation', out=g_i[0:6], in_=pbi[0:6, :], func=AF.Identity, bias=cols[0:6, C_BI:C_BI + 1]), reads=[rpbi, R['cols']], writes=[rG])
                P.op('act', CALL('activation', out=g_e[0:6], in_=pbf[0:6, :], func=AF.Exp, scale=-1.0, bias=cols[0:6, C_BF:C_BF + 1]), reads=[rpbf, R['cols']], writes=[rG])
                P.op('act', CALL('activation', out=g_e[0:6], in_=g_e[0:6], func=AF.Ln, bias=1.0), reads=[rG], writes=[rG])
                for c in range(NB):
                    P.op('dve', CALL('tensor_tensor_scan', out=g_nb[0:6, c * 128:(c + 1) * 128], data0=ones6[0:6, :], data1=g_e[0:6, c * 128:(c + 1) * 128],
                                                                  initial=0.0, op0=ALU.mult, op1=ALU.add), reads=[rG, R['ones6']], writes=[rG])
                P.op('dve', CALL('tensor_tensor', out=g_a[0:6], in0=g_i[0:6], in1=g_nb[0:6], op=ALU.add), reads=[rG], writes=[rG])
                P.op('dve', CALL('tensor_reduce', out=g_sm[0:6, 0:4], in_=g_a[0:6].rearrange("p (c n) -> p c n", n=128), axis=AX.X, op=ALU.max), reads=[rG], writes=[rG])
                P.op('dve', CALL('tensor_scalar', out=g_sm[0:6, 4:8], in0=g_nb[0:6].rearrange("p (c n) -> p c n", n=128)[:, :, 127], scalar1=-1.0, scalar2=None, op0=ALU.mult), reads=[rG], writes=[rG])
                P.op('dve', CALL('tensor_tensor_scan', out=g_sm[0:6, 8:12], data0=g_sm[0:6, 0:4], data1=g_sm[0:6, 4:8], initial=mstate[0:6, 0:1], op0=ALU.max, op1=ALU.add),
                     reads=[rG, R['mstate']], writes=[rG])
                P.op('dve', CALL('tensor_copy', out=g_sm[0:6, 12:13], in_=mstate[0:6, 0:1]), reads=[R['mstate'], rG], writes=[rG])
                P.op('dve', CALL('tensor_copy', out=g_sm[0:6, 13:16], in_=g_sm[0:6, 8:11]), reads=[rG], writes=[rG])
                P.op('dve', CALL('tensor_tensor', out=g_sm[0:6, 16:20], in0=g_sm[0:6, 12:16], in1=g_sm[0:6, 0:4], op=ALU.max), reads=[rG], writes=[rG])
                P.op('dve', CALL('tensor_copy', out=mstate[0:6, 0:1], in_=g_sm[0:6, 11:12]), reads=[rG], writes=[R['mstate']])
                P.op('dve', CALL('tensor_tensor', out=g_sm[0:6, 20:24], in0=g_sm[0:6, 12:16], in1=g_sm[0:6, 16:20], op=ALU.subtract), reads=[rG], writes=[rG])
                P.op('act', CALL('activation', out=g_sm[0:6, 24:28], in_=g_sm[0:6, 20:24], func=AF.Exp), reads=[rG], writes=[rG])
                P.op('dve', CALL('tensor_tensor', out=g_w[0:6].rearrange("p (c n) -> p c n", n=128), in0=g_a[0:6].rearrange("p (c n) -> p c n", n=128),
                                                      in1=g_sm[0:6, 16:20].unsqueeze(2).to_broadcast([6, NB, 128]), op=ALU.subtract), reads=[rG], writes=[rG])
                P.op('dve', CALL('tensor_scalar', out=g_w[0:6], in0=g_w[0:6], scalar1=LNSC, scalar2=None, op0=ALU.add), reads=[rG], writes=[rG])
                P.op('act', CALL('activation', out=g_w[0:6], in_=g_w[0:6], func=AF.Exp), reads=[rG], writes=[rG])
                P.op('dve', CALL('tensor_tensor', out=g_fl[0:6].rearrange("p (c n) -> p c n", n=128), in0=g_nb[0:6].rearrange("p (c n) -> p c n", n=128),
                                                      in1=g_sm[0:6, 16:20].unsqueeze(2).to_broadcast([6, NB, 128]), op=ALU.subtract), reads=[rG], writes=[rG])
                P.op('act', CALL('activation', out=g_fl[0:6], in_=g_fl[0:6], func=AF.Exp), reads=[rG], writes=[rG])
                pg, rpg = bank()
                for c in range(NB):
                    P.op('pe', CALL('transpose', out=pg[:, c * 16:c * 16 + 6], in_=g_w[0:6, c * 128:(c + 1) * 128], identity=identf[0:6, 0:6]), reads=[rG, R['identf']], writes=[rpg])
                    P.op('pe', CALL('transpose', out=pg[:, c * 16 + 8:c * 16 + 14], in_=g_fl[0:6, c * 128:(c + 1) * 128], identity=identf[0:6, 0:6]), reads=[rG, R['identf']], writes=[rpg])
                rGT = ares('gateT')
                P.op('act', CALL('copy', out=gateT.rearrange("p c (t s) -> p c t s", s=8)[:, :, :, 0:6], in_=pg[:, 0:NB * 16].rearrange("p (c t s) -> p c t s", t=2, s=8)[:, :, :, 0:6]),
                     reads=[rpg], writes=[rGT])
                P.op('dve', CALL('tensor_tensor', out=Dg[0:6], in0=g_sm[0:6, 24:28].unsqueeze(1).to_broadcast([6, 6, NB]),
                                                      in1=identf[0:6, 0:6].unsqueeze(2).to_broadcast([6, 6, NB]), op=ALU.mult), reads=[rG, R['identf']], writes=[ares('Dg')])
                pd, rpd = bank()
                P.op('pe', CALL('matmul', pd[:, 0:6 * NB], lhsT=ones6[0:6, :], rhs=Dg[0:6].rearrange("p a b -> p (a b)"), start=True, stop=True), reads=[ares('Dg'), R['ones6']], writes=[rpd])
                P.op('act', CALL('copy', out=decbc.rearrange("p a b -> p (a b)"), in_=pd[:, 0:6 * NB]), reads=[rpd], writes=[rGT])

                head_base = AR.off
                assert head_base <= OFF_PT, head_base

                def interleave(a, b):
                    da = a is None
                    db = b is None
                    while not (da and db):
                        if not da:
                            try:
                                next(a)
                            except StopIteration:
                                da = True
                        if not db:
                            try:
                                next(b)
                            except StopIteration:
                                db = True

                def drive(pairs):
                    for _ in pairs[0][0]():
                        pass
                    for i in range(len(pairs)):
                        a = pairs[i][1]()
                        b = pairs[i + 1][0]() if i + 1 < len(pairs) else None
                        interleave(a, b)

                def norm_a(pn, rpn, cp, tmp):
                    numsb = tmp['numsb'][cp]
                    sm = stat[:, 8 + 16 * cp:24 + 16 * cp]
                    rS = R['stat_h%d' % cp]
                    rn = ares('numsb%d' % cp)
                    P.op('dve', CALL('tensor_scalar', out=numsb, in0=pn[:, 0:256], scalar1=1.0, scalar2=None, op0=ALU.mult, op1=ALU.add, accum_out=sm[:, 0:1]),
                         reads=[rpn], writes=[rn, rS])
                    P.op('act', CALL('activation', out=junk2[:, 256 * cp:256 * cp + 256], in_=numsb, func=AF.Square, accum_out=sm[:, 1:2]), reads=[rn], writes=[R['junk%d' % cp], rS])

                def norm_b(ddcol, cp, tmp):
                    numsb, hn = tmp['numsb'][cp], tmp['hn'][cp]
                    sm = stat[:, 8 + 16 * cp:24 + 16 * cp]
                    rS = R['stat_h%d' % cp]
                    rn = ares('numsb%d' % cp)
                    P.op('dve', CALL('scalar_tensor_tensor', out=sm[:, 2:3], in0=sm[:, 0:1], scalar=-1.0 / 65536, in1=sm[:, 0:1], op0=ALU.mult, op1=ALU.mult), reads=[rS], writes=[rS])
                    if ddcol is not None:
                        P.op('dve', CALL('scalar_tensor_tensor', out=sm[:, 3:4], in0=ddcol, scalar=EPS, in1=ddcol, op0=ALU.mult, op1=ALU.mult), reads=[rS], writes=[rS])
                        P.op('dve', CALL('tensor_tensor', out=sm[:, 2:3], in0=sm[:, 2:3], in1=sm[:, 3:4], op=ALU.add), reads=[rS], writes=[rS])
                    else:
                        P.op('dve', CALL('tensor_scalar', out=sm[:, 2:3], in0=sm[:, 2:3], scalar1=EPS, scalar2=None, op0=ALU.add), reads=[rS], writes=[rS])
                    P.op('dve', CALL('scalar_tensor_tensor', out=sm[:, 4:5], in0=sm[:, 1:2], scalar=1.0 / 256, in1=sm[:, 2:3], op0=ALU.mult, op1=ALU.add), reads=[rS], writes=[rS])
                    P.op('pool', CALL('tensor_tensor', out=sm[:, 6:7], in0=sm[:, 4:5], in1=mhalf[:], op=ALU.pow), reads=[rS, R['mhalf']], writes=[rS])
                    P.op('dve', CALL('scalar_tensor_tensor', out=sm[:, 7:8], in0=sm[:, 0:1], scalar=-1.0 / 256, in1=sm[:, 6:7], op0=ALU.mult, op1=ALU.mult), reads=[rS], writes=[rS])
                    P.op('act', CALL('activation', out=hn, in_=numsb, func=AF.Identity, scale=sm[:, 6:7], bias=sm[:, 7:8]), reads=[rn, rS], writes=[ares('hn%d' % cp)])

                def norm_tail(cp, tmp, dst, rdst, gcol0, Gt, rGt, c):
                    hn = tmp['hn'][cp]
                    pt, rpt = bank()
                    ptb = pt[:].bitcast(BF16)
                    for half in range(2):
                        P.op('pe', CALL('transpose', out=ptb[:, half * 128:(half + 1) * 128], in_=hn[:, half * 128:(half + 1) * 128], identity=identb[:]),
                             reads=[ares('hn%d' % cp), R['identb']], writes=[rpt])
                    for half in range(2):
                        P.op('dve', CALL('scalar_tensor_tensor', out=dst[:, half, c * 128:(c + 1) * 128], in0=ptb[:, half * 128:(half + 1) * 128],
                                         scalar=cols[:, gcol0 + half:gcol0 + half + 1], in1=Gt[:, half, c * 128:(c + 1) * 128],
                                         op0=ALU.mult, op1=ALU.mult), reads=[rpt, R['cols'], rGt], writes=[rdst])

                def fm_proj(fb):
                    wap, rw = getw(('fm', fb))
                    pb, rpb = mm_fm(wap, rw, lambda kc: hT[:, kc, :], [R['hT']], KC)
                    relw('wfm')
                    return pb, rpb

                def tm_proj(key, c0, dst_fn, rdst):
                    wap, rw = getw(key)
                    bap, rb = load_brow(c0)
                    for c in range(NB):
                        pb, rpb = bank()
                        for kc in range(KC):
                            P.op('pe', CALL('matmul', pb[:, 0:256], lhsT=hT[:, kc, c * 128:(c + 1) * 128], rhs=wap[:, kc, :], start=(kc == 0), stop=(kc == KC - 1)),
                                 reads=[rw, R['hT']], writes=[rpb])
                        P.op('dve', CALL('tensor_tensor', out=dst_fn(c), in0=pb[:, 0:256], in1=bap, op=ALU.add), reads=[rpb, rb], writes=[rdst])
                        yield
                    relw('wtm')

                ckpt('B')
                AR.reset(OFF_PT)
                c_uq = AR.alloc([T + 3], F32)
                c_uk = AR.alloc([T + 3], F32)
                c_yq = AR.alloc([T], F32)
                c_yk = AR.alloc([T], F32)
                c_tO = AR.alloc([2, T], BF16)
                c_sZ = AR.alloc([2, T], BF16)
                assert AR.off <= OFF_NT, AR.off
                AR.reset(OFF_NT)
                tmp = dict(numsb=[AR.alloc([256], F32) for _ in range(2)], hn=[AR.alloc([256], BF16) for _ in range(2)],
                           sTm=[AR.alloc([128], BF16) for _ in range(2)])
                assert AR.off <= OFF_ET, AR.off
                c_live = []
                for par_ in range(2):
                    AR.reset(OFF_L0 + par_ * LIVE_SZ)
                    c_live.append(dict(qT=AR.alloc([T], BF16), kT=AR.alloc([T], BF16), kw=AR.alloc([NB, 128], BF16), vx=AR.alloc([NB, 257], BF16),
                                       Gt=AR.alloc([2, T], BF16)))
                    assert AR.off <= OFF_L0 + (par_ + 1) * LIVE_SZ, AR.off

                def c_proj(h):
                    par = h % 2
                    if tt == 0:
                        pump(12)
                    L = c_live[par]
                    for qi, (fb, u, y, nm) in enumerate([(FB_MLQ + h, c_uq, c_yq, 'q'), (FB_MLK + h, c_uk, c_yk, 'k')]):
                        dstT = L[nm + 'T']
                        pb, rpb = fm_proj(fb)
                        cb = qi * 6 + h
                        ru = ares('u' + nm)
                        P.op('dve', CALL('tensor_copy', out=u[:, 0:3], in_=carry[:, cb, :]), reads=[R['carry']], writes=[ru])
                        P.op('act', CALL('activation', out=u[:, 3:T + 3], in_=pb[:, 0:T], func=AF.Identity, bias=cols[:, C_BFM + fb:C_BFM + fb + 1]),
                             reads=[rpb, R['cols']], writes=[ru])
                        P.op('dve', CALL('tensor_copy', out=carry[:, cb, :], in_=u[:, T:T + 3]), reads=[ru], writes=[R['carry']])
                        ry = ares('y' + nm)
                        P.op('dve', CALL('tensor_scalar', out=y, in0=u[:, 3:T + 3], scalar1=cols[:, C_CONVW + cb * 4 + 3:C_CONVW + cb * 4 + 4],
                                         scalar2=cols[:, C_CONVB + cb:C_CONVB + cb + 1], op0=ALU.mult, op1=ALU.add), reads=[ru, R['cols']], writes=[ry])
                        for k in range(3):
                            P.op('dve', CALL('scalar_tensor_tensor', out=y, in0=u[:, k:k + T], scalar=cols[:, C_CONVW + cb * 4 + k:C_CONVW + cb * 4 + k + 1],
                                             in1=y, op0=ALU.mult, op1=ALU.add), reads=[ru, R['cols'], ry], writes=[ry])
                        P.op('act', CALL('activation', out=dstT, in_=y, func=AF.Silu), reads=[ry], writes=[ares('%sT%d' % (nm, par))])
                        yield
                    vx = L['vx']
                    P.op('dve', CALL('memset', vx[:, :, 256:257], 1.0), writes=[ares('vx%d' % par)])
                    yield from tm_proj(('tm', 'mlv', h * 256), h * 256, lambda c: vx[:, c, 0:256], ares('vx%d' % par))
                    for half in range(2):
                        fbo = FB_MLO + 2 * h + half
                        pb, rpb = fm_proj(fbo)
                        P.op('act', CALL('activation', out=c_tO[:, half, :], in_=pb[:, 0:T], func=AF.Tanh, scale=0.5, bias=cols[:, C_HBFM + fbo:C_HBFM + fbo + 1]),
                             reads=[rpb, R['cols']], writes=[ares('tO')])
                        yield
                        fbz = FB_MLZ + 2 * h + half
                        pb, rpb = fm_proj(fbz)
                        P.op('act', CALL('activation', out=c_sZ[:, half, :], in_=pb[:, 0:T], func=AF.Silu, bias=cols[:, C_BFM + fbz:C_BFM + fbz + 1]),
                             reads=[rpb, R['cols']], writes=[ares('sZ')])
                        yield
                    P.op('dve', CALL('scalar_tensor_tensor', out=L['Gt'], in0=c_tO, scalar=1.0, in1=c_sZ, op0=ALU.add, op1=ALU.mult), reads=[ares('tO'), ares('sZ')], writes=[ares('Gt%d' % par)])
                    pk, rpk = bank()
                    pkb = pk[:].bitcast(BF16)
                    kT, kw = L['kT'], L['kw']
                    for c in range(NB):
                        P.op('pe', CALL('transpose', out=pkb[:, c * 128:(c + 1) * 128], in_=kT[:, c * 128:(c + 1) * 128], identity=identb[:]), reads=[ares('kT%d' % par), R['identb']], writes=[rpk])
                    for c in range(NB):
                        P.op('act', CALL('activation', out=kw[:, c, :], in_=pkb[:, c * 128:(c + 1) * 128], func=AF.Identity, scale=gateT[:, c, h:h + 1]), reads=[rpk, rGT], writes=[ares('kw%d' % par)])
                    yield

                def c_rec(h):
                    par = h % 2
                    L = c_live[par]
                    qT, kT, kw, vx, Gt = L['qT'], L['kT'], L['kw'], L['vx'], L['Gt']
                    rq, rk, rkw, rvx, rGt_ = ares('qT%d' % par), ares('kT%d' % par), ares('kw%d' % par), ares('vx%d' % par), ares('Gt%d' % par)

                    def front(c):
                        cp = c % 2
                        cs = slice(c * 128, (c + 1) * 128)
                        ps_s, rps = bank()
                        P.op('pe', CALL('matmul', ps_s[:, 0:128], lhsT=kT[:, cs], rhs=qT[:, cs], start=True, stop=True), reads=[rk, rq], writes=[rps])
                        P.op('dve', CALL('scalar_tensor_tensor', out=tmp['sTm'][cp], in0=ps_s[:, 0:128], scalar=gateT[:, c, h:h + 1], in1=maskT[:], op0=ALU.mult, op1=ALU.mult),
                             reads=[rps, rGT, R['maskT']], writes=[ares('sTm%d' % cp)])
                        P.op('act', CALL('activation', out=Cb[:], in_=Cst[:, h, :], func=AF.Identity, scale=decbc[:, h, c:c + 1]), reads=[r_C[h], rGT], writes=[R['Cb']])

                    front(0)
                    yield
                    pending = None
                    for c in range(NB):
                        cp = c % 2
                        cs = slice(c * 128, (c + 1) * 128)
                        sTm = tmp['sTm'][cp]
                        rsT = ares('sTm%d' % cp)
                        pn, rpn = bank()
                        P.op('pe', CALL('matmul', pn[:, 0:257], lhsT=sTm, rhs=vx[:, c, :], start=True, stop=False), reads=[rsT, rvx], writes=[rpn])
                        P.op('pe', CALL('matmul', pn[:, 0:257], lhsT=qT[:, cs], rhs=Cb[:], start=False, stop=True), reads=[rq, R['Cb']], writes=[rpn])
                        pkv, rpkv = bank()
                        P.op('pe', CALL('matmul', pkv[:, 0:257], lhsT=kw[:, c, :], rhs=vx[:, c, :], start=True, stop=True), reads=[rkw, rvx], writes=[rpkv])
                        P.op('dve', CALL('scalar_tensor_tensor', out=Cst[:, h, :], in0=Cst[:, h, :], scalar=decbc[:, h, c:c + 1], in1=pkv[:, 0:257], op0=ALU.mult, op1=ALU.add),
                             reads=[r_C[h], rGT, rpkv], writes=[r_C[h]])
                        dcol = 40 + 2 * cp
                        rS = R['stat_h%d' % cp]
                        P.op('act', CALL('activation', out=stat[:, dcol:dcol + 1], in_=pn[:, 256:257], func=AF.Abs), reads=[rpn], writes=[rS])
                        norm_a(pn, rpn, cp, tmp)
                        if c + 1 < NB:
                            front(c + 1)
                        yield
                        P.op('dve', CALL('tensor_tensor', out=stat[:, dcol:dcol + 1], in0=stat[:, dcol:dcol + 1], in1=gateT[:, c, 8 + h:9 + h], op=ALU.max), reads=[rS, rGT], writes=[rS])
                        norm_b(stat[:, dcol:dcol + 1], cp, tmp)
                        if pending is not None:
                            norm_tail(*pending)
                        pending = (cp, tmp, mlo[:, 2 * h:2 * h + 2, :], R['mlo'], C_MLG + 2 * h, Gt, rGt_, c)
                        yield
                    norm_tail(*pending)
                    yield

                AR.reset(OFF_PT)
                d_tq = AR.alloc([NB, 256], F32)
                d_rp1 = AR.alloc([2, 64], F32)
                d_rp2 = AR.alloc([2, 64], F32)
                d_qkr = AR.alloc([NB, 256], BF16)
                d_qf = AR.alloc([T], F32)
                assert AR.off <= OFF_NT, AR.off
                d_live = []
                for par_ in range(2):
                    AR.reset(OFF_L0 + par_ * LIVE_SZ)
                    d_live.append(dict(kd=AR.alloc([NB, 128], BF16), qT=AR.alloc([T], BF16), kT=AR.alloc([T], BF16), qdT=AR.alloc([T], BF16),
                                       vv=AR.alloc([NB, 256], BF16), sZ=AR.alloc([2, T], BF16)))
                    assert AR.off <= OFF_L0 + (par_ + 1) * LIVE_SZ, AR.off

                def d_proj(h):
                    par = h % 2
                    if tt == 0:
                        pump(12)
                    L = d_live[par]
                    gen = tm_proj(('tm', 'rtqk', 1536 + h * 256), 1536 + h * 256, lambda c: d_tq[:, c, :], ares('tq'))
                    for c in range(NB):
                        next(gen)
                        t4 = d_tq[:, c, :].rearrange("p (a b c) -> p a b c", a=2, b=2)
                        o4 = d_qkr[:, c, :].rearrange("p (a b c) -> p a b c", a=2, b=2)
                        cosb = cosT[:, c, :].unsqueeze(1).to_broadcast([128, 2, 64])
                        sinb = sinT[:, c, :].unsqueeze(1).to_broadcast([128, 2, 64])
                        rr, rr2 = ares('rope'), ares('rope2')
                        P.op('dve', CALL('tensor_tensor', out=d_rp1, in0=t4[:, :, 0, :], in1=cosb, op=ALU.mult), reads=[ares('tq'), R['cosT']], writes=[rr])
                        P.op(ROPE_ENG, CALL('tensor_tensor', out=d_rp2, in0=t4[:, :, 1, :], in1=sinb, op=ALU.mult), reads=[ares('tq'), R['sinT']], writes=[rr2])
                        P.op('dve', CALL('tensor_tensor', out=o4[:, :, 0, :], in0=d_rp1, in1=d_rp2, op=ALU.subtract), reads=[rr, rr2], writes=[ares('qk_r')])
                        P.op('dve', CALL('tensor_tensor', out=d_rp1, in0=t4[:, :, 0, :], in1=sinb, op=ALU.mult), reads=[ares('tq'), R['sinT']], writes=[rr])
                        P.op(ROPE_ENG, CALL('tensor_tensor', out=d_rp2, in0=t4[:, :, 1, :], in1=cosb, op=ALU.mult), reads=[ares('tq'), R['cosT']], writes=[rr2])
                        P.op('dve', CALL('tensor_tensor', out=o4[:, :, 1, :], in0=d_rp1, in1=d_rp2, op=ALU.add), reads=[rr, rr2], writes=[ares('qk_r')])
                        P.op('act', CALL('activation', out=L['kd'][:, c, :], in_=d_qkr[:, c, 128:256], func=AF.Identity, scale=cols[:, C_KDEC + h:C_KDEC + h + 1]),
                             reads=[ares('qk_r'), R['cols']], writes=[ares('d_kd%d' % par)])
                        yield
                    for _ in gen:
                        pass
                    vv = L['vv']
                    yield from tm_proj(('tm', 'rtv', 3072 + h * 256), 3072 + h * 256, lambda c: vv[:, c, :], ares('d_vv%d' % par))
                    for half in range(2):
                        fbz = FB_RTZ + 2 * h + half
                        pb, rpb = fm_proj(fbz)
                        P.op('act', CALL('activation', out=L['sZ'][:, half, :], in_=pb[:, 0:T], func=AF.Silu, bias=cols[:, C_BFM + fbz:C_BFM + fbz + 1]),
                             reads=[rpb, R['cols']], writes=[ares('d_sZ%d' % par)])
                        yield
                    pq, rpq = bank()
                    pqb = pq[:].bitcast(BF16)
                    for c in range(NB):
                        P.op('pe', CALL('transpose', out=pqb[:, c * 128:(c + 1) * 128], in_=d_qkr[:, c, 0:128], identity=identb[:]), reads=[ares('qk_r'), R['identb']], writes=[rpq])
                        P.op('pe', CALL('transpose', out=pqb[:, T + c * 128:T + (c + 1) * 128], in_=d_qkr[:, c, 128:256], identity=identb[:]), reads=[ares('qk_r'), R['identb']], writes=[rpq])
                    P.op('act', CALL('copy', out=L['qT'], in_=pqb[:, 0:T]), reads=[rpq], writes=[ares('d_qT%d' % par)])
                    P.op('act', CALL('copy', out=L['kT'], in_=pqb[:, T:2 * T]), reads=[rpq], writes=[ares('d_kT%d' % par)])
                    P.op('act', CALL('copy', out=d_qf, in_=pqb[:, 0:T]), reads=[rpq], writes=[ares('qf32')])
                    P.op('dve', CALL('tensor_tensor', out=L['qdT'].rearrange("p (c n) -> p c n", n=128), in0=d_qf.rearrange("p (c n) -> p c n", n=128),
                                     in1=qdec[:, h, :].unsqueeze(1).to_broadcast([128, NB, 128]), op=ALU.mult), reads=[ares('qf32'), R['qdec']], writes=[ares('d_qdT%d' % par)])
                    yield

                def d_rec(h):
                    par = h % 2
                    L = d_live[par]
                    qT, kT, kd, qdT, vv, sZ = L['qT'], L['kT'], L['kd'], L['qdT'], L['vv'], L['sZ']
                    rq, rk, rkd, rqd, rvv, rsz = [ares('d_%s%d' % (n_, par)) for n_ in ('qT', 'kT', 'kd', 'qdT', 'vv', 'sZ')]

                    def front(c):
                        cp = c % 2
                        cs = slice(c * 128, (c + 1) * 128)
                        ps_s, rps = bank()
                        P.op('pe', CALL('matmul', ps_s[:, 0:128], lhsT=kT[:, cs], rhs=qT[:, cs], start=True, stop=True), reads=[rk, rq], writes=[rps])
                        P.op('dve', CALL('tensor_tensor', out=tmp['sTm'][cp], in0=ps_s[:, 0:128], in1=intraT[:, h, :], op=ALU.mult), reads=[rps, R['intraT']], writes=[ares('sTm%d' % cp)])
                        P.op('act', CALL('copy', out=Rb[:], in_=Rst[:, h, :]), reads=[r_R[h]], writes=[R['Rb']])

                    front(0)
                    yield
                    pending = None
                    for c in range(NB):
                        cp = c % 2
                        cs = slice(c * 128, (c + 1) * 128)
                        sTm = tmp['sTm'][cp]
                        rsT = ares('sTm%d' % cp)
                        pn, rpn = bank()
                        P.op('pe', CALL('matmul', pn[:, 0:256], lhsT=sTm, rhs=vv[:, c, :], start=True, stop=False), reads=[rsT, rvv], writes=[rpn])
                        P.op('pe', CALL('matmul', pn[:, 0:256], lhsT=qdT[:, cs], rhs=Rb[:], start=False, stop=True), reads=[rqd, R['Rb']], writes=[rpn])
                        pkv, rpkv = bank()
                        P.op('pe', CALL('matmul', pkv[:, 0:256], lhsT=kd[:, c, :], rhs=vv[:, c, :], start=True, stop=True), reads=[rkd, rvv], writes=[rpkv])
                        P.op('dve', CALL('scalar_tensor_tensor', out=Rst[:, h, :], in0=Rst[:, h, :], scalar=cols[:, C_CD + h:C_CD + h + 1], in1=pkv[:, 0:256], op0=ALU.mult, op1=ALU.add),
                             reads=[r_R[h], R['cols'], rpkv], writes=[r_R[h]])
                        norm_a(pn, rpn, cp, tmp)
                        if c + 1 < NB:
                            front(c + 1)
                        yield
                        norm_b(None, cp, tmp)
                        if pending is not None:
                            norm_tail(*pending)
                        pending = (cp, tmp, rto[:, 2 * h:2 * h + 2, :], R['rto'], C_RTG + 2 * h, sZ, rsz, c)
                        yield
                    norm_tail(*pending)
                    yield

                AR.reset(OFF_ET)
                e_pp = [AR.alloc([256], F32) for _ in range(2)]
                e_pnb = [AR.alloc([256], BF16) for _ in range(2)]
                assert AR.off <= OFF_L0, AR.off
                e_live = []
                for par_ in range(2):
                    AR.reset(OFF_L0 + par_ * LIVE_SZ)
                    e_live.append(dict(xqT=AR.alloc([2, T], BF16), xsz=AR.alloc([2, T], BF16), pT=AR.alloc([2, T], BF16)))

                def e_proj(h):
                    par = h % 2
                    if tt == 0:
                        pump(12)
                    L = e_live[par]
                    for half in range(2):
                        fb = FB_XAQ + 2 * h + half
                        pb, rpb = fm_proj(fb)
                        P.op('act', CALL('activation', out=L['xqT'][:, half, :], in_=pb[:, 0:T], func=AF.Identity, bias=cols[:, C_BFM + fb:C_BFM + fb + 1]),
                             reads=[rpb, R['cols']], writes=[ares('xqT%d' % par)])
                        yield
                    for half in range(2):
                        fb = FB_XAZ + 2 * h + half
                        pb, rpb = fm_proj(fb)
                        P.op('act', CALL('activation', out=L['xsz'][:, half, :], in_=pb[:, 0:T], func=AF.Silu, bias=cols[:, C_BFM + fb:C_BFM + fb + 1]),
                             reads=[rpb, R['cols']], writes=[ares('xsz%d' % par)])
                        yield

                def e_rec(h):
                    par = h % 2
                    L = e_live[par]
                    xqT, xsz, pT = L['xqT'], L['xsz'], L['pT']
                    rxq, rxs, rpT = ares('xqT%d' % par), ares('xsz%d' % par), ares('pT%d' % par)
                    for c in range(NB):
                        cp = c % 2
                        cs = slice(c * 128, (c + 1) * 128)
                        pp, pnb = e_pp[cp], e_pnb[cp]
                        rS = R['stat_x%d' % cp]
                        psc, rpsc = bank()
                        for half in range(2):
                            P.op('pe', CALL('matmul', psc[:, 0:256], lhsT=xqT[:, half, cs], rhs=mkT[:, 2 * h + half, :], start=(half == 0), stop=(half == 1)),
                                 reads=[rxq, R['mkT']], writes=[rpsc])
                        sm = stat[:, 48 + 4 * cp:52 + 4 * cp]
                        P.op('dve', CALL('tensor_reduce', out=sm[:, 0:1], in_=psc[:, 0:256], axis=AX.X, op=ALU.max), reads=[rpsc], writes=[rS])
                        P.op('dve', CALL('tensor_scalar', out=sm[:, 1:2], in0=sm[:, 0:1], scalar1=-1.0 / 16, scalar2=None, op0=ALU.mult), reads=[rS], writes=[rS])
                        P.op('act', CALL('activation', out=pp, in_=psc[:, 0:256], func=AF.Exp, scale=1.0 / 16, bias=sm[:, 1:2], accum_out=sm[:, 2:3]),
                             reads=[rpsc, rS], writes=[ares('pp%d' % cp), rS])
                        P.op('dve', CALL('reciprocal', out=sm[:, 3:4], in_=sm[:, 2:3]), reads=[rS], writes=[rS])
                        P.op('dve', CALL('tensor_scalar', out=pnb, in0=pp, scalar1=sm[:, 3:4], scalar2=None, op0=ALU.mult), reads=[ares('pp%d' % cp), rS], writes=[ares('pnb%d' % cp)])
                        yield
                        ptp, rptp = bank()
                        ptpb = ptp[:].bitcast(BF16)
                        for mh in range(2):
                            P.op('pe', CALL('transpose', out=ptpb[:, mh * 128:(mh + 1) * 128], in_=pnb[:, mh * 128:(mh + 1) * 128], identity=identb[:]),
                                 reads=[ares('pnb%d' % cp), R['identb']], writes=[rptp])
                        P.op('act', CALL('copy', out=pT[:, :, cs], in_=ptpb[:, 0:256].rearrange("p (a b) -> p a b", b=128)), reads=[rptp], writes=[rpT])
                        yield
                    for dh in range(2):
                        po, rpo = bank()
                        for mh in range(2):
                            P.op('pe', CALL('matmul', po[:, 0:T], lhsT=mvv[:, mh, h * 256 + dh * 128:h * 256 + (dh + 1) * 128], rhs=pT[:, mh, :], start=(mh == 0), stop=(mh == 1)),
                                 reads=[R['mvv'], rpT], writes=[rpo])
                        P.op('dve', CALL('tensor_tensor', out=xao[:, 2 * h + dh, :], in0=po[:, 0:T], in1=xsz[:, dh, :], op=ALU.mult), reads=[rpo, rxs], writes=[R['xao']])
                        yield

                stages = ([('C', (lambda h=h: c_proj(h)), (lambda h=h: c_rec(h))) for h in range(6)]
                          + [('D', (lambda h=h: d_proj(h)), (lambda h=h: d_rec(h))) for h in range(6)]
                          + [('E', (lambda h=h: e_proj(h)), (lambda h=h: e_rec(h))) for h in range(4)])
                CE = ('pe', 'act', 'dve', 'pool')

                def snapshot():
                    snap = []
                    for f in CE:
                        n = len(P.ops[f])
                        while n > 0 and (P.ops[f][n - 1]['fn'] is None or P.ops[f][n - 1]['dma'] is not None):
                            n -= 1
                        if n > 0:
                            snap.append((f, n))
                    return snap

                prev_st = [last_store[k_] for k_ in (2, 3) if k_ in last_store]
                if prev_st:
                    for e_ in CE:
                        P.wait_events(e_, prev_st)
                for _ in stages[0][1]():
                    pass
                for i_ in range(len(stages)):
                    snap = snapshot()
                    a_ = stages[i_][2]()
                    b_ = None
                    if i_ + 1 < len(stages):
                        if stages[i_ + 1][0] != stages[i_][0] or (i_ >= 1 and stages[i_ + 1][0] != stages[i_ - 1][0]):
                            for e_ in CE:
                                P.wait_events(e_, snap)
                        b_ = stages[i_ + 1][1]()
                    interleave(a_, b_)
                ckpt('E')
                snapF = snapshot()
                P.barrier()
                P.wait_events('sp', snapF)
                AR.reset()
                A.clear()
                mgT = AR.alloc([KC, T], BF16)
                tg = AR.alloc([T], BF16)
                m0 = AR.alloc([T], F32)
                m1 = AR.alloc([T], F32)
                assert AR.off <= OFF_ET, AR.off
                AR.reset(OFF_L0)
                ybuf = [AR.alloc([D], F32), AR.alloc([D], F32)]
                srcs = {'ml': (mlo, R['mlo'], 12), 'rt': (rto, R['rto'], 12), 'xa': (xao, R['xao'], 8)}
                for n in range(16):
                    for gi, which in enumerate(['ml', 'rt', 'xa']):
                        src, rsrc, nk = srcs[which]
                        wap, rw = getw(('br', which, n))
                        pbp, rpbp = mm_fm(wap, rw, lambda kc, src=src: src[:, kc, :], [rsrc], nk)
                        relw('wfm')
                        fb = FB_G + gi * 16 + n
                        wap, rw = getw(('fm', fb))
                        pbg, rpbg = mm_fm(wap, rw, lambda kc: hT[:, kc, :], [R['hT']], KC)
                        relw('wfm')
                        P.op('act', CALL('activation', out=tg, in_=pbg[:, 0:T], func=AF.Tanh, scale=0.5, bias=cols[:, C_HBFM + fb:C_HBFM + fb + 1]),
                             reads=[rpbg, R['cols']], writes=[ares('tg')])
                        dstm = m0 if gi == 0 else m1
                        rdst = ares('m0') if gi == 0 else ares('m1')
                        P.op('dve', CALL('scalar_tensor_tensor', out=dstm, in0=tg, scalar=1.0, in1=pbp[:, 0:T], op0=ALU.add, op1=ALU.mult),
                             reads=[ares('tg'), rpbp], writes=[rdst])
                        if gi == 1:
                            P.op('pool', CALL('tensor_tensor', out=m0, in0=m0, in1=m1, op=ALU.add), reads=[ares('m0'), ares('m1')], writes=[ares('m0')])
                        if gi == 2:
                            P.op('pool', CALL('tensor_tensor', out=mgT[:, n, :], in0=m0, in1=m1, op=ALU.add), reads=[ares('m0'), ares('m1')], writes=[ares('mgT')])
                def outproj(pair):
                    blks = [2 * pair, 2 * pair + 1]
                    xsl = []
                    for j, c in enumerate(blks):
                        if pair == 0:
                            xs, rx = load_rows(x_d[tok0 + c * 128:tok0 + (c + 1) * 128, :])
                            sl = (xt_ctr[0] - 1) % 2
                            xsl.append((c, xs, rx, s_xt[sl], sl))
                        else:
                            xs, rx = ybuf[j], r_y[j]
                            P.op('sp', CALL('dma_start', out=xs, in_=x_d[tok0 + c * 128:tok0 + (c + 1) * 128, :]), writes=[rx], dma_sem=s_y[j])
                            xsl.append((c, xs, rx, s_y[j], 2 + j))
                    for q in range(8):
                        wap, rw = getw(('out', q))
                        for (c, xs, rx, sem_, key_) in xsl:
                            pb, rpb = bank()
                            for kc in range(KC):
                                P.op('pe', CALL('matmul', pb[:, 0:256], lhsT=mgT[:, kc, c * 128:(c + 1) * 128], rhs=wap[:, kc, :], start=(kc == 0), stop=(kc == KC - 1)),
                                     reads=[rw, ares('mgT')], writes=[rpb])
                            P.op('dve', CALL('scalar_tensor_tensor', out=xs[:, q * 256:(q + 1) * 256], in0=pb[:, 0:256], scalar=0.5, in1=xs[:, q * 256:(q + 1) * 256], op0=ALU.mult, op1=ALU.add),
                                 reads=[rpb, rx], writes=[rx])
                        relw('wtm')
                        yield
                    for (c, xs, rx, sem_, key_) in xsl:
                        sc = 4 + 2 * (c % 2)
                        ss = stat[:, sc:sc + 1]
                        rs = stat[:, sc + 1:sc + 2]
                        P.op('act', CALL('activation', out=junk[:], in_=xs, func=AF.Square, accum_out=ss), reads=[rx], writes=[R['junk'], R['stat_a']])
                        P.op('dve', CALL('tensor_scalar', out=rs, in0=ss, scalar1=1.0 / D, scalar2=EPS, op0=ALU.mult, op1=ALU.add), reads=[R['stat_a']], writes=[R['stat_a']])
                        P.op('pool', CALL('tensor_tensor', out=rs, in0=rs, in1=mhalf[:], op=ALU.pow), reads=[R['stat_a'], R['mhalf']], writes=[R['stat_a']])
                        P.op('dve', CALL('scalar_tensor_tensor', out=xs, in0=xs, scalar=rs, in1=fgbc[:], op0=ALU.mult, op1=ALU.mult), reads=[rx, R['stat_a'], R['fgbc']], writes=[rx])
                        P.op('sp', CALL('dma_start', out=out_d[tok0 + c * 128:tok0 + (c + 1) * 128, :], in_=xs), reads=[rx], writes=[], dma_sem=sem_)
                        last_store[key_] = (('d', sem_), 16 * P.dma_cnt[sem_])
                        yield

                for _ in outproj(0):
                    pass
                go_ = outproj(1)
                ga_ = phaseA(tt + 1) if tt + 1 < NT else None
                for ch_ in 'ooooaoaaoaooaoaaoaa':
                    g_ = go_ if ch_ == 'o' else ga_
                    if g_ is not None:
                        try:
                            next(g_)
                        except StopIteration:
                            pass
                interleave(go_, ga_)
                P.barrier()

        try:
            body()
        except _Stop:
            pass
        P.wait_events('sp', [v for v in last_store.values()])
        assert kstop is not None or SM.cur == len(SM.specs), (SM.cur, len(SM.specs))
        P.emit(st)
    return nc


_CACHE = {}


def _consts():
    h = np.arange(6)
    gam = 1.0 - 2.0 ** (-5.0 - h)
    lg = np.log(gam)
    l = np.arange(128)
    maskT = (l[:, None] <= l[None, :]).astype(np.float32)
    intraT = np.zeros((128, 6, 128), np.float64)
    for hh in range(6):
        d = (l[None, :] - l[:, None])
        intraT[:, hh, :] = np.where(d >= 0, np.exp(d * lg[hh]), 0.0) * (128.0 ** -0.5)
    qdec = np.zeros((128, 6, 128), np.float64)
    for hh in range(6):
        qdec[:, hh, :] = np.exp((l + 1.0) * lg[hh])[None, :]
    kdec = np.exp((127.0 - l)[:, None] * lg[None, :]) * (128.0 ** -0.5)
    cd = np.broadcast_to(np.exp(128.0 * lg)[None, :], (128, 6))
    freqs = (10000.0 ** (-np.arange(64, dtype=np.float32) / np.float32(64))).astype(np.float32)
    return dict(maskT=maskT, intraT=intraT.reshape(128, 768).astype(np.float32), qdec=qdec.reshape(128, 768).astype(np.float32),
                kdec=kdec.astype(np.float32), cd=np.ascontiguousarray(cd).astype(np.float32),
                freq=np.ascontiguousarray(np.broadcast_to(freqs[None, :], (128, 64))).astype(np.float32))


def _pack(inputs):
    cst = _consts()
    b_in = np.asarray(inputs["b_in"], np.float32)[0]
    cols = np.zeros((128, NCOLS), np.float32)
    for b, off in enumerate(FM_OFFS):
        cols[:, C_BFM + b] = b_in[off:off + 128]
    cw = np.asarray(inputs["conv_w"], np.float32)[0]
    cbias = np.asarray(inputs["conv_b"], np.float32)[0]
    for cb in range(12):
        for k in range(4):
            cols[:, C_CONVW + cb * 4 + k] = cw[k, cb * 128:(cb + 1) * 128]
        cols[:, C_CONVB + cb] = cbias[cb * 128:(cb + 1) * 128]
    cols[:, C_MLG:C_MLG + 12] = np.asarray(inputs["ml_hnorm_g"], np.float32)[0].reshape(12, 128).T
    cols[:, C_RTG:C_RTG + 12] = np.asarray(inputs["ret_hnorm_g"], np.float32)[0].reshape(12, 128).T
    cols[:, C_GCOL:C_GCOL + 16] = np.asarray(inputs["ln_g"], np.float32)[0].reshape(16, 128).T
    cols[:, C_MGCOL:C_MGCOL + 16] = np.asarray(inputs["mem_ln_g"], np.float32)[0].reshape(16, 128).T
    cols[0:6, C_BI] = b_in[O_MLI:O_MLI + 6]
    cols[0:6, C_BF] = b_in[O_MLF:O_MLF + 6]
    cols[:, C_KDEC:C_KDEC + 6] = cst["kdec"]
    cols[:, C_CD:C_CD + 6] = cst["cd"]
    rows = np.zeros((1, NTM), np.float32)
    for (dc, sc, n) in TM_SEGS:
        rows[0, dc:dc + n] = b_in[sc:sc + n]
    shared = dict(
        w_in=np.ascontiguousarray(np.asarray(inputs["w_in"], np.float32)[0]),
        w_kv=np.ascontiguousarray(np.asarray(inputs["w_mem_kv"], np.float32)[0]),
        w_ml=np.ascontiguousarray(np.asarray(inputs["w_br_ml"], np.float32)[0]),
        w_rt=np.ascontiguousarray(np.asarray(inputs["w_br_ret"], np.float32)[0]),
        w_xa=np.ascontiguousarray(np.asarray(inputs["w_br_xa"], np.float32)[0]),
        w_out=np.ascontiguousarray(np.asarray(inputs["w_out"], np.float32)[0]),
        cols=cols, rows=rows,
        fg=np.ascontiguousarray(np.asarray(inputs["final_g"], np.float32).reshape(1, D)),
        identb=np.eye(128).astype(ml_dtypes.bfloat16), identf=np.eye(128, dtype=np.float32),
        maskT=cst["maskT"], intraT=cst["intraT"], qdec=cst["qdec"], freq=cst["freq"],
    )
    return shared


def make_in_maps(inputs):
    x = np.asarray(inputs["x"], np.float32)
    mem = np.asarray(inputs["mem"], np.float32)
    pos = np.asarray(inputs["positions"], np.int32)
    B, S, _ = x.shape
    shared = _pack(inputs)
    in_maps = []
    for b in range(B):
        m = dict(shared)
        m["x"] = np.ascontiguousarray(x[b])
        m["mem"] = np.ascontiguousarray(mem[b])
        m["pos"] = np.ascontiguousarray(pos[b].reshape(S // 128, 128).T)
        in_maps.append(m)
    return in_maps, B, S


def kernel(**inputs):
    in_maps, B, S = make_in_maps(inputs)
    if S not in _CACHE:
        _CACHE[S] = build_program(S)
    nc = _CACHE[S]
    res = run_bass_kernel_spmd(nc, in_maps, core_ids=list(range(B)))
    out = np.stack([np.asarray(r["out"], np.float32) for r in res.results], axis=0)
    return out
```
